# Optimizing a Trainium2 kernel written in Bass

```python
import jax, jax.numpy as jnp
from jax import lax
import numpy as np

D_MODEL = 1024
BATCH = 8
SEQ = 4096
DEPTH = 2

HEAD_DIM = 64
N_HEADS_FOX = 8
N_HEADS_MOBA = 8
MIX_WIDTH = (N_HEADS_FOX + N_HEADS_MOBA) * HEAD_DIM
FOX_QKV_W = 3 * N_HEADS_FOX * HEAD_DIM
MOBA_QKV_W = 3 * N_HEADS_MOBA * HEAD_DIM
EVEN_IN_W = FOX_QKV_W + N_HEADS_FOX + MOBA_QKV_W + MIX_WIDTH
Q_BLOCK = 128
MOBA_BLOCK = 256
MOBA_TOPK = 3
ROPE_THETA = 500000.0
ROT_DIM = HEAD_DIM // 4
CONV_WIDTH = 31
CONV_CH = D_MODEL
NORM_EPS = 1e-6
LN_EPS = 1e-5

kernel_name = "fox_moba_conformer_hybrid"


def rms_norm(x, g):
    xf = x.astype(jnp.float32)
    y = xf * lax.rsqrt(jnp.mean(xf * xf, axis=-1, keepdims=True) + NORM_EPS)
    return (y * g.astype(jnp.float32)).astype(x.dtype)


def layer_norm(x, g, b):
    xf = x.astype(jnp.float32)
    mu = jnp.mean(xf, axis=-1, keepdims=True)
    xc = xf - mu
    y = xc * lax.rsqrt(jnp.mean(xc * xc, axis=-1, keepdims=True) + LN_EPS)
    return (y * g.astype(jnp.float32) + b.astype(jnp.float32)).astype(x.dtype)


def partial_rope(x, pos):
    half = ROT_DIM // 2
    inv_freq = ROPE_THETA ** (-jnp.arange(half, dtype=jnp.float32) / half)
    ang = pos.astype(jnp.float32)[:, None] * inv_freq[None, :]
    cos = jnp.cos(ang)[None, :, None, :]
    sin = jnp.sin(ang)[None, :, None, :]
    xf = x[..., :ROT_DIM].astype(jnp.float32)
    x1, x2 = xf[..., :half], xf[..., half:]
    rot = jnp.concatenate([x1 * cos - x2 * sin, x2 * cos + x1 * sin], axis=-1).astype(x.dtype)
    return jnp.concatenate([rot, x[..., ROT_DIM:]], axis=-1)


def fox_attention(q, k, v, logf):
    B, H, S, D = q.shape
    nqb = S // Q_BLOCK
    scale = D ** -0.5
    c = jnp.cumsum(logf, axis=-1)
    q_blocks = q.reshape(B, H, nqb, Q_BLOCK, D).transpose(2, 0, 1, 3, 4)
    c_blocks = c.reshape(B, H, nqb, Q_BLOCK).transpose(2, 0, 1, 3)
    k_pos = jnp.arange(S)

    def one_block(args):
        qb, cb, blk = args
        q_pos = blk * Q_BLOCK + jnp.arange(Q_BLOCK)
        logits = jnp.einsum('bhqd,bhkd->bhqk', qb, k).astype(jnp.float32) * scale
        logits = logits + cb[..., None] - c[:, :, None, :]
        logits = jnp.where(k_pos[None, :] <= q_pos[:, None], logits, -jnp.inf)
        p = jax.nn.softmax(logits, axis=-1)
        return jnp.einsum('bhqk,bhkd->bhqd', p.astype(v.dtype), v)

    out = lax.map(one_block, (q_blocks, c_blocks, jnp.arange(nqb)))
    return out.transpose(1, 0, 3, 2, 4).reshape(B, S, H, D)


def moba_attention(q, k, v):
    B, H, S, D = q.shape
    nqb = S // Q_BLOCK
    nblk = -(-S // MOBA_BLOCK)
    s_pad = nblk * MOBA_BLOCK
    scale = D ** -0.5
    pad = ((0, 0), (0, 0), (0, s_pad - S), (0, 0))
    kb = jnp.pad(k, pad).reshape(B, H, nblk, MOBA_BLOCK, D)
    vb = jnp.pad(v, pad).reshape(B, H, nblk, MOBA_BLOCK, D)
    kmean = jnp.mean(kb.astype(jnp.float32), axis=3)
    ksel = min(MOBA_TOPK, nblk)
    h_idx = jnp.arange(H)[:, None, None]
    blk_ids = jnp.arange(nblk)
    in_blk = jnp.arange(MOBA_BLOCK)

    def one_chunk(n):
        b = n // nqb
        q0 = (n % nqb) * Q_BLOCK
        own = q0 // MOBA_BLOCK
        qc = lax.dynamic_slice(q, (b, 0, q0, 0), (1, H, Q_BLOCK, D))[0]
        kb_b = lax.dynamic_index_in_dim(kb, b, 0, keepdims=False)
        vb_b = lax.dynamic_index_in_dim(vb, b, 0, keepdims=False)
        km_b = lax.dynamic_index_in_dim(kmean, b, 0, keepdims=False)
        gate = jnp.einsum('hqd,hnd->hqn', qc.astype(jnp.float32), km_b)
        gate = jnp.where(blk_ids[None, None, :] < own, gate, -jnp.inf)
        _, idx = lax.top_k(gate, ksel)
        valid = idx < own
        kg = kb_b[h_idx, idx]
        vg = vb_b[h_idx, idx]
        lp = jnp.einsum('hqd,hqnld->hqnl', qc, kg).astype(jnp.float32) * scale
        lp = jnp.where(valid[..., None], lp, -jnp.inf).reshape(H, Q_BLOCK, ksel * MOBA_BLOCK)
        k_own = lax.dynamic_index_in_dim(kb_b, own, 1, keepdims=False)
        v_own = lax.dynamic_index_in_dim(vb_b, own, 1, keepdims=False)
        lo = jnp.einsum('hqd,hld->hql', qc, k_own).astype(jnp.float32) * scale
        q_pos = q0 + jnp.arange(Q_BLOCK)
        key_pos = own * MOBA_BLOCK + in_blk
        lo = jnp.where(key_pos[None, :] <= q_pos[:, None], lo, -jnp.inf)
        p = jax.nn.softmax(jnp.concatenate([lp, lo], axis=-1), axis=-1)
        pp = p[..., :ksel * MOBA_BLOCK].reshape(H, Q_BLOCK, ksel, MOBA_BLOCK).astype(v.dtype)
        po = p[..., ksel * MOBA_BLOCK:].astype(v.dtype)
        return (jnp.einsum('hqnl,hqnld->hqd', pp, vg)
                + jnp.einsum('hql,hld->hqd', po, v_own))

    out = lax.map(one_chunk, jnp.arange(B * nqb))
    return out.reshape(B, nqb, H, Q_BLOCK, D).transpose(0, 1, 3, 2, 4).reshape(B, S, H, D)


def fox_moba_layer(x, norm_g, w_in, b_f, qn_fox, kn_fox, qn_moba, kn_moba, w_out):
    B, S, _ = x.shape
    h = rms_norm(x, norm_g)
    proj = h @ w_in
    fox_qkv, f_logit, moba_qkv, gate = jnp.split(
        proj, [FOX_QKV_W, FOX_QKV_W + N_HEADS_FOX, FOX_QKV_W + N_HEADS_FOX + MOBA_QKV_W], axis=-1)
    fox_qkv = fox_qkv.reshape(B, S, 3, N_HEADS_FOX, HEAD_DIM)
    qa = rms_norm(fox_qkv[:, :, 0], qn_fox)
    ka = rms_norm(fox_qkv[:, :, 1], kn_fox)
    va = fox_qkv[:, :, 2]
    logf = jax.nn.log_sigmoid((f_logit + b_f).astype(jnp.float32))
    out_a = fox_attention(qa.transpose(0, 2, 1, 3), ka.transpose(0, 2, 1, 3),
                          va.transpose(0, 2, 1, 3), logf.transpose(0, 2, 1))
    pos = jnp.arange(S)
    moba_qkv = moba_qkv.reshape(B, S, 3, N_HEADS_MOBA, HEAD_DIM)
    qb = partial_rope(rms_norm(moba_qkv[:, :, 0], qn_moba), pos)
    kb = partial_rope(rms_norm(moba_qkv[:, :, 1], kn_moba), pos)
    vb = moba_qkv[:, :, 2]
    out_b = moba_attention(qb.transpose(0, 2, 1, 3), kb.transpose(0, 2, 1, 3),
                           vb.transpose(0, 2, 1, 3))
    mix = jnp.concatenate([out_a, out_b], axis=2).reshape(B, S, MIX_WIDTH)
    return x + (mix * jax.nn.silu(gate)) @ w_out


def conformer_conv_layer(x, norm_g, w_in, conv_w, conv_b, ln_g, ln_b, w_out):
    h = rms_norm(x, norm_g)
    val, glu_gate, z = jnp.split(h @ w_in, 3, axis=-1)
    u = val * jax.nn.sigmoid(glu_gate)
    u = lax.conv_general_dilated(
        u, conv_w, window_strides=(1,), padding=((CONV_WIDTH - 1, 0),),
        dimension_numbers=('NWC', 'WIO', 'NWC'), feature_group_count=CONV_CH) + conv_b
    u = jax.nn.silu(layer_norm(u, ln_g, ln_b)) * jax.nn.silu(z)
    return x + u @ w_out


def setup_inputs(seed: int = 0) -> dict:
    key = jax.random.key(seed)
    ks = jax.random.split(key, 16)
    f32 = jnp.float32
    nrm = lambda k, shape, s: jax.random.normal(k, shape, f32) * s
    return {
        "x": jax.random.normal(ks[0], (BATCH, SEQ, D_MODEL), f32),
        "l0_norm": 1.0 + nrm(ks[1], (D_MODEL,), 0.02),
        "l0_w_in": nrm(ks[2], (D_MODEL, EVEN_IN_W), D_MODEL ** -0.5),
        "l0_b_f": jnp.linspace(2.0, 7.0, N_HEADS_FOX, dtype=f32) + nrm(ks[3], (N_HEADS_FOX,), 0.1),
        "l0_qn_fox": 1.0 + nrm(ks[4], (HEAD_DIM,), 0.02),
        "l0_kn_fox": 1.0 + nrm(ks[5], (HEAD_DIM,), 0.02),
        "l0_qn_moba": 1.0 + nrm(ks[6], (HEAD_DIM,), 0.02),
        "l0_kn_moba": 1.0 + nrm(ks[7], (HEAD_DIM,), 0.02),
        "l0_w_out": nrm(ks[8], (MIX_WIDTH, D_MODEL), MIX_WIDTH ** -0.5),
        "l1_norm": 1.0 + nrm(ks[9], (D_MODEL,), 0.02),
        "l1_w_in": nrm(ks[10], (D_MODEL, 3 * CONV_CH), D_MODEL ** -0.5),
        "l1_conv_w": nrm(ks[11], (CONV_WIDTH, 1, CONV_CH), CONV_WIDTH ** -0.5),
        "l1_conv_b": nrm(ks[12], (CONV_CH,), 0.02),
        "l1_ln_g": 1.0 + nrm(ks[13], (CONV_CH,), 0.02),
        "l1_ln_b": nrm(ks[14], (CONV_CH,), 0.02),
        "l1_w_out": nrm(ks[15], (CONV_CH, D_MODEL), CONV_CH ** -0.5),
    }


def reference(x, l0_norm, l0_w_in, l0_b_f, l0_qn_fox, l0_kn_fox, l0_qn_moba, l0_kn_moba, l0_w_out,
              l1_norm, l1_w_in, l1_conv_w, l1_conv_b, l1_ln_g, l1_ln_b, l1_w_out):
    even_params = [(l0_norm, l0_w_in, l0_b_f, l0_qn_fox, l0_kn_fox, l0_qn_moba, l0_kn_moba, l0_w_out)]
    odd_params = [(l1_norm, l1_w_in, l1_conv_w, l1_conv_b, l1_ln_g, l1_ln_b, l1_w_out)]
    for layer in range(DEPTH):
        if layer % 2 == 0:
            x = fox_moba_layer(x, *even_params[layer // 2])
        else:
            x = conformer_conv_layer(x, *odd_params[layer // 2])
    return x
```

```python
import numpy as np
import ml_dtypes
import concourse.bass as bass
import concourse.mybir as mybir
from concourse.bass_utils import run_bass_kernel_spmd

F32 = mybir.dt.float32
BF16 = mybir.dt.bfloat16
AF = mybir.ActivationFunctionType
ALU = mybir.AluOpType
AX = mybir.AxisListType

NCORES = 8
D = 1024
HD = 64
NEG = -30000.0


DEBUG_LINES = None


class Res:
    __slots__ = ("w", "r", "name", "const")

    def __init__(self, name="", const=False):
        self.w = None
        self.r = []
        self.name = name
        self.const = const


class Op:
    __slots__ = ("eng", "fn", "dma", "deps", "cost", "idx", "cnt", "dsem", "dval", "prev_slot",
                 "signal", "start", "finish", "nbytes", "waits", "ndesc", "line")

    def __init__(self, eng, fn, dma, cost, nbytes=0):
        self.eng = eng
        self.fn = fn
        self.dma = dma
        self.deps = {}
        self.cost = cost
        self.nbytes = nbytes
        self.signal = False
        self.cnt = None
        self.dsem = None
        self.dval = None
        self.prev_slot = None
        self.start = None
        self.finish = None
        self.waits = ()
        self.ndesc = 128


class Sched:
    ENGS = ("pe", "act", "dve", "pool", "sp")
    CE = ("pe", "act", "dve", "pool")
    NDMA = 12
    WINDOW = 192
    LAT = 300.0
    MAXDESC = 1024

    def __init__(self, nc):
        self.nc = nc
        self.ops = {e: [] for e in self.ENGS}
        self.sem = {e: nc.alloc_semaphore("s_" + e) for e in self.CE}
        self.dsems = {e: [nc.alloc_semaphore("d_%s%d" % (e, i)) for i in range(self.NDMA)]
                      for e in ("sp", "pool", "act")}
        self.nops = 0
        self.since_bar = []
        self.bar = {}

    @staticmethod
    def _need_wait(d, o, kind):
        if d.dma or o.dma or d.eng != o.eng:
            return True
        return o.eng != "pe"

    def _add(self, o):
        o.idx = self.nops
        self.nops += 1
        b = self.bar.get(o.eng)
        if b is not None and id(b) not in o.deps:
            o.deps[id(b)] = (b, False)
        self.ops[o.eng].append(o)
        self.since_bar.append(o)

    def op(self, eng, fn, reads=(), writes=(), dma=False, cost=300.0, nbytes=0):
        o = Op(eng, fn, dma, cost, nbytes)
        if DEBUG_LINES is not None:
            import sys as _sys
            f = _sys._getframe(1)
            chain = []
            while f is not None and len(chain) < 4:
                chain.append(f.f_lineno)
                f = f.f_back
            o.line = chain

        def dep(d, kind):
            if d is o:
                return
            nw = self._need_wait(d, o, kind)
            old = o.deps.get(id(d))
            o.deps[id(d)] = (d, nw or (old[1] if old else False))

        for r in reads:
            if r.w is not None:
                dep(r.w, "raw")
        for w in writes:
            if w.w is not None:
                dep(w.w, "raw")
            for rd in w.r:
                dep(rd, "war")
        for r in reads:
            if not r.const:
                r.r.append(o)
        for w in writes:
            w.w = o
            w.r = []
        self._add(o)
        return o

    def barrier(self):
        prior = list(self.since_bar)
        self.since_bar = []
        newbar = {}
        for e in self.ENGS:
            o = Op(e, lambda eng: eng.nop(), False, 50.0)
            for d in prior:
                o.deps[id(d)] = (d, d.dma or d.eng != e)
            newbar[e] = o
        for e in self.ENGS:
            self.bar[e] = None
            self._add(newbar[e])
        self.bar = newbar
        self.since_bar = []

    def schedule(self):
        ENGS = self.ENGS
        LAT = self.LAT
        users = {}
        for e in ENGS:
            for o in self.ops[e]:
                o.start = None
                o.finish = None
                o.cnt = None
                o.signal = False
                o.waits = [len(o.deps), 0.0]
                for d, nw in o.deps.values():
                    users.setdefault(id(d), []).append((o, nw))
        unsched = {e: list(self.ops[e]) for e in ENGS}
        pos = {e: 0 for e in ENGS}
        free = {e: 0.0 for e in ENGS}
        order = {e: [] for e in ENGS}
        bus = 0.0
        total = sum(len(v) for v in unsched.values())
        done = 0
        W = self.WINDOW
        while done < total:
            best = None
            for e in ENGS:
                lst = unsched[e]
                p = pos[e]
                n = len(lst)
                while p < n and lst[p] is None:
                    p += 1
                pos[e] = p
                seen = 0
                q = p
                fe = free[e]
                while q < n and seen < W:
                    o = lst[q]
                    q += 1
                    if o is None:
                        continue
                    seen += 1
                    w = o.waits
                    if w[0]:
                        continue
                    st = w[1] if w[1] > fe else fe
                    if best is None or (st, o.idx) < best[0]:
                        best = ((st, o.idx), e, q - 1, o)
                    if st <= fe:
                        break
            key, e, q, o = best
            st = key[0]
            o.start = st
            if o.dma:
                t0 = st if st > bus else bus
                bus = t0 + o.nbytes / 160.0
                o.finish = bus + 2000.0
                free[e] = st + 60.0
            else:
                o.finish = st + o.cost
                free[e] = o.finish
            for u, nw in users.get(id(o), ()):
                uw = u.waits
                uw[0] -= 1
                if nw:
                    t = o.finish + (LAT if (o.eng != u.eng or o.dma) else 0.0)
                else:
                    t = o.start
                if t > uw[1]:
                    uw[1] = t
            unsched[e][q] = None
            order[e].append(o)
            done += 1
        self.makespan = max(free.values())
        return order

    def emit(self, block, reorder=True):
        order = self.schedule() if reorder else {e: list(self.ops[e]) for e in self.ENGS}
        qpos = {}
        for e in self.ENGS:
            for i, o in enumerate(order[e]):
                qpos[id(o)] = i
        all_dma = []
        for e in self.ENGS:
            k = 0
            last = [None] * self.NDMA
            val = [0] * self.NDMA
            for o in order[e]:
                if not o.dma:
                    continue
                slot = k % self.NDMA
                k += 1
                o.prev_slot = last[slot]
                last[slot] = o
                val[slot] += 16
                o.dsem = self.dsems[e][slot]
                o.dval = val[slot]
                all_dma.append(o)
        for e in self.ENGS:
            fifo, fifo_n = [], [0]
            for o in order[e]:
                latest = {}
                dl = []
                for d, nw in o.deps.values():
                    if not nw:
                        continue
                    if d.dma:
                        dl.append(d)
                    else:
                        c = latest.get(d.eng)
                        if c is None or qpos[id(d)] > qpos[id(c)]:
                            latest[d.eng] = d
                if o.dma and o.prev_slot is not None:
                    dl.append(o.prev_slot)
                if o.dma:
                    while fifo and fifo_n[0] + o.ndesc > self.MAXDESC:
                        old = fifo.pop(0)
                        fifo_n[0] -= old.ndesc
                        dl.append(old)
                    fifo.append(o)
                    fifo_n[0] += o.ndesc
                dmax = {}
                for d in dl:
                    c = dmax.get(id(d.dsem))
                    if c is None or d.dval > c.dval:
                        dmax[id(d.dsem)] = d
                o.waits = list(latest.values()) + list(dmax.values())
                for d in latest.values():
                    d.signal = True
        tails = []
        for ce in self.CE:
            comp = [o for o in order[ce] if not o.dma]
            if comp:
                comp[-1].signal = True
                tails.append(comp[-1])
        for e in self.CE:
            c = 0
            for o in order[e]:
                if not o.dma and o.signal:
                    c += 1
                    o.cnt = c

        def run(e, engobj):
            waited = {}

            def wait(d):
                if d.dma:
                    key, val, sem = ("d", id(d.dsem)), d.dval, d.dsem
                else:
                    key, val, sem = ("c", d.eng), d.cnt, self.sem[d.eng]
                if waited.get(key, 0) >= val:
                    return
                waited[key] = val
                engobj.wait_ge(sem, val)

            for o in order[e]:
                for d in o.waits:
                    wait(d)
                ins = o.fn(engobj)
                if DEBUG_LINES is not None:
                    try:
                        DEBUG_LINES[str(ins.ins.name)] = o.line
                    except Exception:
                        pass
                if o.dma:
                    ins.then_inc(o.dsem, 16)
                elif o.signal:
                    ins.then_inc(self.sem[e], 1)
            if e == "sp":
                for d in all_dma:
                    wait(d)
                for d in tails:
                    wait(d)

        @block.tensor
        def _(eng):
            run("pe", eng)

        @block.scalar
        def _(eng):
            run("act", eng)

        @block.vector
        def _(eng):
            run("dve", eng)

        @block.gpsimd
        def _(eng):
            run("pool", eng)

        @block.sync
        def _(eng):
            run("sp", eng)


def host_consts(S):
    NT = S // 128
    bf = ml_dtypes.bfloat16
    ident = np.eye(128, dtype=np.float32)
    tri = (np.arange(128)[:, None] <= np.arange(128)[None, :]).astype(np.float32)
    rot = np.zeros((64, 64), np.float32)
    for d in range(8):
        rot[d + 8, d] = -1.0
        rot[d, d + 8] = 1.0
    rot2 = np.zeros((128, 128), np.float32)
    rot2[:64, :64] = rot
    rot2[64:, 64:] = rot
    bo = np.zeros((128, 128), np.float32)
    bo[:64, :64] = 1.0 / 64
    bo[64:, 64:] = 1.0 / 64
    om = np.full((128, 128), 1.0 / 1024, np.float32)
    cbf = np.concatenate([ident, tri, rot2, bo, om], axis=1).astype(bf)
    half = 8
    inv_freq = (500000.0 ** (-np.arange(half, dtype=np.float32) / half)).astype(np.float32)
    ang = np.arange(S, dtype=np.float32)[:, None] * inv_freq[None, :]
    cos = np.cos(ang).astype(np.float32)
    sin = np.sin(ang).astype(np.float32)
    cosT = np.ones((128, S), np.float32)
    sinT = np.zeros((128, S), np.float32)
    for base in (0, 64):
        cosT[base:base + 8] = cos.T
        cosT[base + 8:base + 16] = cos.T
        sinT[base:base + 8] = sin.T
        sinT[base + 8:base + 16] = sin.T
    blk = np.zeros((32, S), np.float32)
    blk[(np.arange(S) // 256), np.arange(S)] = 1.0
    cb = np.zeros((128, NT, 16), np.float32)
    negm = np.full((128, NT, 16), NEG, np.float32)
    for qt in range(NT):
        own = qt // 2
        cb[:, qt, own:] = -1e30
        negm[:, qt, own] = 0.0
    ones3 = np.ones((3, S), np.float32).astype(bf)
    return dict(cbf=cbf, cosT=cosT, sinT=sinT, blk1h=blk.astype(bf),
                cbm=cb.reshape(128, NT * 16), negm=negm.reshape(128, NT * 16), ones3=ones3,
                onesf=np.ones((128, 128), np.float32))


def host_vec(inp):
    def c8(v):
        return np.asarray(v, np.float32).reshape(8, 128).T
    def t2(v):
        return np.tile(np.asarray(v, np.float32), 2).reshape(128, 1)
    bfv = np.zeros((128, 1), np.float32)
    bfv[:8, 0] = np.asarray(inp["l0_b_f"], np.float32)
    cw = np.asarray(inp["l1_conv_w"], np.float32).reshape(31, 8, 128).transpose(2, 1, 0).reshape(128, 248)
    cols = [c8(inp["l0_norm"]), c8(inp["l1_norm"]), t2(inp["l0_qn_fox"]), t2(inp["l0_kn_fox"]),
            t2(inp["l0_qn_moba"]), t2(inp["l0_kn_moba"]), c8(inp["l1_conv_b"]), c8(inp["l1_ln_g"]),
            c8(inp["l1_ln_b"]), bfv, cw]
    return np.ascontiguousarray(np.concatenate(cols, axis=1))
V_G0, V_G1, V_GAIN, V_CB, V_LG, V_LB, V_BF, V_CW, NV = 0, 8, 16, 20, 28, 36, 44, 45, 45 + 248


def build(S, stop_after=None):
    NT = S // 128
    NCH = S // 512
    NB = S // 256
    nc = bass.Bass("TRN2", target_bir_lowering=False)
    di = lambda n, sh, dt=F32: nc.dram_tensor(n, sh, dt, kind="ExternalInput").ap()
    x_d = di("x", [S, D])
    w0_d = di("w_in0", [D, 4104])
    wo0_d = di("w_out0", [D, D])
    w1_d = di("w_in1", [D, 3072])
    wo1_d = di("w_out1", [D, D])
    vec_d = di("vec", [128, NV])
    cbf_d = di("cbf", [128, 640], BF16)
    cos_d = di("cosT", [128, S])
    sin_d = di("sinT", [128, S])
    blk_d = di("blk1h", [32, S], BF16)
    cbm_d = di("cbm", [128, NT * 16])
    negm_d = di("negm", [128, NT * 16])
    ones3_d = di("ones3", [3, S], BF16)
    onesf_d = di("onesf", [128, 128])
    out_d = nc.dram_tensor("out", [S, D], F32, kind="ExternalOutput").ap()
    X1_d = nc.dram_tensor("x1s", [S, D], F32, kind="Internal").ap()
    MG_d = nc.dram_tensor("mgs", [8, 128, S], BF16, kind="Internal").ap()
    CK_d = nc.dram_tensor("cks", [8, 3, S], BF16, kind="Internal").ap()
    CQ_d = nc.dram_tensor("cqs", [8, 3, S], BF16, kind="Internal").ap()

    Sc = Sched(nc)

    sb = lambda n, sh, dt=F32: nc.alloc_sbuf_tensor(n, sh, dt)
    vec = sb("vecs", [128, NV])
    cbf = sb("cbfs", [128, 640], BF16)
    onesf = sb("onesfs", [128, 128])
    gsc = sb("gsc", [128, 4])
    nbf = sb("nbf", [128, 1])
    hT = sb("hT", [128, 8, S], BF16)
    ident, tri, rot2, bones, omean = (cbf[:, i * 128:(i + 1) * 128] for i in range(5))
    ARB = (nc.sbuf_bytes_remaining - 512) // 128 * 128
    arena = sb("arena", [128, ARB // 2], BF16)
    R_vec, R_cbf, R_onesf, R_gsc, R_nbf = Res(), Res(), Res(), Res(), Res()
    R_hT = [Res() for _ in range(NT)]

    class Carver:
        def __init__(self):
            self.off = 0

        def __call__(self, shape, dt=F32):
            esz = 4 if dt == F32 else 2
            n = int(np.prod(shape[1:])) * esz
            n = (n + 63) // 64 * 64
            a = arena[:, self.off // 2:(self.off + n) // 2]
            self.off += n
            assert self.off <= ARB, self.off
            v = a.bitcast(F32) if dt == F32 else a
            nfree = int(np.prod(shape[1:]))
            v = v[0:shape[0], 0:nfree]
            if len(shape) == 3:
                v = v.rearrange("p (a b) -> p a b", b=shape[2])
            return v

    PSB = [nc.alloc_psum_tensor("psb%d" % i, [128, 512], F32) for i in range(8)]
    R_PS = [Res() for _ in range(8)]

    def ncols(ap):
        return int(np.prod(ap.shape[1:]))

    def op(eng, fn, reads=(), writes=(), dma=False, cols=512, cost=None, nbytes=0):
        if cost is None:
            cost = {"pe": 16 + cols / 2.4, "act": 220 + cols / 1.4, "dve": 60 + cols * 1.3,
                    "pool": 100 + cols * 2.4, "sp": 60.0}[eng]
        return Sc.op(eng, fn, reads, writes, dma=dma, cost=cost, nbytes=nbytes)

    def dma(out, in_, reads=(), writes=(), eng="sp", **kw):
        nb = int(np.prod(out.shape)) * (4 if out.dtype == F32 else 2)
        o = op(eng, lambda e: e.dma_start(out=out, in_=in_, **kw), reads, writes, dma=True, nbytes=nb)
        o.ndesc = max(int(np.prod(out.shape[:-1])), int(np.prod(in_.shape[:-1])))
        return o

    def mm(out, lhsT, rhs, start, stop, reads, writes):
        n = ncols(rhs) * (4 if rhs.dtype == F32 else 1)
        return op("pe", lambda e: e.matmul(out, lhsT, rhs, start=start, stop=stop), reads, writes, cols=n)

    def act(out, in_, func, reads, writes, **kw):
        return op("act", lambda e: e.activation(out=out, in_=in_, func=func, **kw), reads, writes, cols=ncols(out))

    def tt(eng, out, a, b, alu, reads, writes):
        return op(eng, lambda e: e.tensor_tensor(out, a, b, alu), reads, writes, cols=ncols(out))

    def ts(eng, out, a, s1, s2, o0, o1, reads, writes):
        if o1 is None:
            return op(eng, lambda e: e.tensor_scalar(out, a, s1, None, o0), reads, writes, cols=ncols(out))
        return op(eng, lambda e: e.tensor_scalar(out, a, s1, s2, o0, o1), reads, writes, cols=ncols(out))

    dma(vec[:], vec_d, writes=[R_vec])
    dma(cbf[:], cbf_d, writes=[R_cbf])
    dma(onesf[:], onesf_d, writes=[R_onesf])
    ts("dve", gsc[:, 0:1], vec[:, V_GAIN:V_GAIN + 1], 0.125, None, ALU.mult, None, [R_vec], [R_gsc])
    op("dve", lambda e: e.tensor_copy(gsc[:, 1:2], vec[:, V_GAIN + 1:V_GAIN + 2]), [R_vec], [R_gsc], cols=1)
    ts("dve", gsc[:, 2:3], vec[:, V_GAIN + 2:V_GAIN + 3], 0.125, None, ALU.mult, None, [R_vec], [R_gsc])
    op("dve", lambda e: e.tensor_copy(gsc[:, 3:4], vec[:, V_GAIN + 3:V_GAIN + 4]), [R_vec], [R_gsc], cols=1)
    ts("dve", nbf[:], vec[:, V_BF:V_BF + 1], -1.0, None, ALU.mult, None, [R_vec], [R_nbf])

    def norm_transpose(ti, xt, R_x, tmp, nset=2, banks=(0, 1)):
        sq, ss, rs, hb, R = tmp
        b = ti % nset
        bkk = banks[ti % len(banks)]
        act(sq[b], xt, AF.Square, [R_x], [R["sq"][b], R["ss"][b]], accum_out=ss[b])
        act(rs[b], ss[b], AF.Sqrt, [R["ss"][b]], [R["rs"][b]], bias=1e-6, scale=1.0 / D)
        op("dve", lambda e: e.reciprocal(rs[b], rs[b]), [R["rs"][b]], [R["rs"][b]], cols=1)
        ts("dve", hb[b], xt, rs[b], None, ALU.mult, None, [R_x, R["rs"][b]], [R["hb"][b]])
        pb = PSB[bkk][:].bitcast(BF16).rearrange("p (a b) -> p a b", b=128)
        for c in range(8):
            op("pe", lambda e, c=c: e.transpose(pb[:, c, :], hb[b][:, c * 128:(c + 1) * 128], ident),
               [R["hb"][b], R_cbf], [R_PS[bkk]], cols=128)
        op("act" if ti % 2 else "dve",
           (lambda e: e.copy(out=hT[:, :, ti * 128:(ti + 1) * 128], in_=pb)) if ti % 2 else
           (lambda e: e.tensor_copy(hT[:, :, ti * 128:(ti + 1) * 128], pb)),
           [R_PS[bkk]], [R_hT[ti]], cols=1024)

    def load_w(dst_bf, src_ap, gcol, stage, R_stage, R_dst, eng="pool"):
        n = dst_bf.shape[2]
        srcv = src_ap.rearrange("(c p) n -> p c n", p=128)
        dma(stage[:, 0:4, 0:n], srcv[:, 0:4, :], writes=[R_stage])
        dma(stage[:, 4:8, 0:n], srcv[:, 4:8, :], writes=[R_stage])
        if gcol is None:
            op(eng, lambda e: e.tensor_copy(dst_bf, stage[:, :, 0:n]), [R_stage], [R_dst])
        else:
            g = vec[:, gcol:gcol + 8].unsqueeze(2).to_broadcast([128, 8, n])
            tt(eng, dst_bf, stage[:, :, 0:n], g, ALU.mult, [R_stage, R_vec], [R_dst])

    cv = Carver()
    xt_ = [cv([128, D]) for _ in range(2)]
    R_xt = [Res(), Res()]
    tmpA = ([cv([128, D], BF16) for _ in range(2)], [cv([128, 1]) for _ in range(2)],
            [cv([128, 1]) for _ in range(2)], [cv([128, D], BF16) for _ in range(2)],
            {k: [Res(), Res()] for k in ("sq", "ss", "rs", "hb")})
    NSA = 4 if S >= 4096 else 2
    if NSA == 4:
        tmpA[1].extend([cv([128, 1]) for _ in range(2)])
        tmpA[2].extend([cv([128, 1]) for _ in range(2)])
        for k_ in ("sq", "ss", "rs", "hb"):
            tmpA[4][k_].extend([Res(), Res()])
        R_xt.extend([Res(), Res()])
    off_common = cv.off
    wst = [cv([128, 8, 128]) for _ in range(2)]
    R_wst = [Res(), Res()]
    wb = [cv([128, 8, 128], BF16) for _ in range(4)]
    R_wb = [Res() for _ in range(4)]
    wfb = cv([128, 8, 8], BF16)
    R_wfb = Res()
    NPT = 5
    PT = [cv([128, 512], BF16) for _ in range(NPT)]
    R_PT = [Res() for _ in range(NPT)]
    NM = 8
    misc = [cv([128, 512]) for _ in range(NM)]
    R_misc = [Res() for _ in range(NM)]
    sqb = [cv([128, 512], BF16) for _ in range(3)]
    R_sqb = [Res() for _ in range(3)]
    qnbs = [cv([128, 512], BF16) for _ in range(3)]
    R_qnbs = [Res() for _ in range(3)]
    cs_t = [(cv([128, 512]), cv([128, 512])) for _ in range(2)]
    R_cs = [Res(), Res()]
    Vpe = cv([128, NT, 65], BF16)
    Vpo = cv([128, NT, 128], BF16)
    R_Vp = Res()
    CBt = cv([128, NT, 16])
    NEGMt = cv([128, NT, 16])
    Gt = cv([128, NT, 16])
    LTt = cv([128, NT, 16])
    top8 = cv([128, NT, 8])
    mbp = cv([128, NT, 32], BF16)
    kmf = cv([128, 16])
    kmb = cv([128, 16], BF16)
    R_CB, R_G, R_LT, R_mbp, R_km = Res(), Res(), Res(), Res(), Res()
    R_top8 = [Res() for _ in range(NT)]
    off_R1 = cv.off
    E_t = cv([8, S])
    off_CN = cv.off
    CN_t = cv([8, S])
    TA_t = cv([8, S], BF16)
    TB_t = cv([8, S], BF16)
    R_E, R_CN, R_TA, R_TB = Res(), Res(), Res(), Res()
    R_CKd, R_CQd = Res(), Res()
    if NSA == 4:
        cv2 = Carver()
        cv2.off = off_CN
        for _ in range(2):
            xt_.append(cv2([128, D]))
            tmpA[0].append(cv2([128, D], BF16))
            tmpA[3].append(cv2([128, D], BF16))
        assert cv2.off <= off_CN + 8 * S * 4 // 8 * 8 and cv2.off - off_CN <= 4 * S
    for ti in range(NT):
        b = ti % NSA
        dma(xt_[b], x_d[ti * 128:(ti + 1) * 128, :], writes=[R_xt[b]])
        norm_transpose(ti, xt_[b], R_xt[b], tmpA, nset=NSA, banks=(0, 1, 4, 5))

    dma(CBt.rearrange("p a b -> p (a b)"), cbm_d, writes=[R_CB])
    dma(NEGMt.rearrange("p a b -> p (a b)"), negm_d, writes=[R_CB])
    op("pool", lambda e: e.memset(Vpe[:, :, 64:65], 1.0), [], [R_Vp])
    op("pool", lambda e: e.memset(Vpo[:, :, 0:64], 0.0), [], [R_Vp])
    op("pool", lambda e: e.memset(Vpo[:, :, 0:1], 1.0), [], [R_Vp])
    op("pool", lambda e: e.memset(mbp[:], 0.0), [], [R_mbp])
    op("pool", lambda e: e.memset(kmb[:], 0.0), [], [R_km])

    wfv = w0_d[:, 1536:1544].rearrange("(c p) n -> p c n", p=128)
    dma(wst[0][:, 0:4, 0:8], wfv[:, 0:4, :], writes=[R_wst[0]])
    dma(wst[0][:, 4:8, 0:8], wfv[:, 4:8, :], writes=[R_wst[0]])
    tt("pool", wfb, wst[0][:, :, 0:8], vec[:, V_G0:V_G0 + 8].unsqueeze(2).to_broadcast([128, 8, 8]), ALU.mult,
       [R_wst[0], R_vec], [R_wfb])
    for tc in range(NCH):
        b = tc % 2
        for c in range(8):
            mm(PSB[b][0:8, :], wfb[:, c, :], hT[:, c, tc * 512:(tc + 1) * 512], c == 0, c == 7,
               [R_wfb] + R_hT[tc * 4:tc * 4 + 4], [R_PS[b]])
        act(E_t[:, tc * 512:(tc + 1) * 512], PSB[b][0:8, :], AF.Exp, [R_PS[b], R_nbf], [R_E],
            bias=nbf[0:8, :], scale=-1.0)
    act(E_t, E_t, AF.Ln, [R_E], [R_E], bias=1.0, scale=1.0)
    op("pool", lambda e: e.memset(TA_t, 1.0), [], [R_TA], cols=S)
    op("dve", lambda e: e.tensor_tensor_scan(CN_t, TA_t, E_t, 0.0, ALU.mult, ALU.add), [R_TA, R_E], [R_CN], cols=2 * S)
    srcs = [(CN_t, R_CN, E_t, R_E), (E_t, R_E, CN_t, R_CN), (CN_t, R_CN, None, None)]
    for p, (src, R_src, dst, R_dst) in enumerate(srcs):
        op("dve", lambda e, src=src: e.tensor_copy(TA_t, src), [R_src], [R_TA], cols=S)
        dma(CK_d[:, p, :], TA_t, reads=[R_TA], writes=[R_CKd])
        ts("dve", TB_t, TA_t, -1.0, None, ALU.mult, None, [R_TA], [R_TB])
        dma(CQ_d[:, p, :], TB_t, reads=[R_TB], writes=[R_CQd])
        if dst is not None:
            tt("dve", dst, src, TA_t, ALU.subtract, [R_src, R_TA], [R_dst])
    Sc.barrier()

    cv.off = off_R1
    QSq = cv([128, S], BF16)
    QSk = cv([128, S], BF16)
    QTo = cv([128, S], BF16)
    KTo = cv([128, S], BF16)
    SG = cv([128, S], BF16)
    MBT = cv([32, S], BF16)
    R_QSq_lo, R_QSq_hi, R_QSk_lo, R_QSk_hi = Res(), Res(), Res(), Res()
    R_QTo_d, R_QTo_x, R_KTo_d, R_KTo_x, R_SG, R_MBT = Res(), Res(), Res(), Res(), Res(), Res()
    R_MGd = [[Res() for _ in range(NCH)] for _ in range(8)]

    mi = [0]

    def M():
        mi[0] = (mi[0] + 1) % NM
        return misc[mi[0]], R_misc[mi[0]]

    pj = [0]
    PJB = [0, 1, 4]

    def proj_bank():
        pj[0] = (pj[0] + 1) % 3
        return PJB[pj[0]]

    def project_fm(wt, R_w, tc):
        b = proj_bank()
        for c in range(8):
            mm(PSB[b][:], wt[:, c, :], hT[:, c, tc * 512:(tc + 1) * 512], c == 0, c == 7,
               [R_w] + R_hT[tc * 4:tc * 4 + 4], [R_PS[b]])
        return b

    uq = [0]

    def qk_chunk(wt, R_w, tc, gcol, dst, R_dst, rope, load_cs=True):
        b = project_fm(wt, R_w, tc)
        uq[0] += 1
        u = uq[0]
        s = u % 3
        sb2 = (2, 5)[u % 2]
        sb3 = (3, 6)[u % 2]
        act(sqb[s], PSB[b][:], AF.Square, [R_PS[b]], [R_sqb[s]])
        mm(PSB[sb2][:], bones, sqb[s], True, True, [R_cbf, R_sqb[s]], [R_PS[sb2]])
        sd, R_sd = M()
        act(sd, PSB[sb2][:], AF.Ln, [R_PS[sb2]], [R_sd], bias=1e-6, scale=1.0)
        act(sd, sd, AF.Exp, [R_sd], [R_sd], scale=-0.5)
        sl = slice(tc * 512, (tc + 1) * 512)
        if not rope:
            op("dve", lambda e: e.scalar_tensor_tensor(dst[:, sl], PSB[b][:], gsc[:, gcol:gcol + 1], sd,
                                                       ALU.mult, ALU.mult), [R_PS[b], R_gsc, R_sd], [R_dst])
            return
        qn, R_qn = M()
        op("dve", lambda e: e.scalar_tensor_tensor(qn, PSB[b][:], gsc[:, gcol:gcol + 1], sd,
                                                   ALU.mult, ALU.mult), [R_PS[b], R_gsc, R_sd], [R_qn])
        qnb = qnbs[s]
        op("act", lambda e: e.copy(out=qnb, in_=qn), [R_qn], [R_qnbs[s]])
        mm(PSB[sb3][:], rot2, qnb, True, True, [R_cbf, R_qnbs[s]], [R_PS[sb3]])
        c2 = tc % 2
        ct, st = cs_t[c2]
        if load_cs:
            dma(ct, cos_d[:, sl], writes=[R_cs[c2]])
            dma(st, sin_d[:, sl], writes=[R_cs[c2]])
        t1, R_t1 = M()
        tt("dve", t1, qn, ct, ALU.mult, [R_qn, R_cs[c2]], [R_t1])
        t2, R_t2 = M()
        tt("dve", t2, PSB[sb3][:], st, ALU.mult, [R_PS[sb3], R_cs[c2]], [R_t2])
        tt("pool", dst[:, sl], t1, t2, ALU.add, [R_t1, R_t2], [R_dst])

    OB = [7, 3]
    SBK = [4, 5, 6, 0, 1]

    def attention(pp, QT, R_Qd, R_Qx, KT, R_Kd, R_Kx, odd):
        Vp = Vpo if odd else Vpe
        vM = 128 if odd else 65
        rows = slice(64, 128) if odd else slice(0, 64)
        rsr = 0 if odd else 64
        tasks = [(i, j) for i in range(NCH) for j in range(4 * i + 4)]
        pend = []

        def issue_S(n):
            i, j = tasks[n]
            jj = j - 4 * i
            lo = 128 * jj if jj > 0 else 0
            sbk = SBK[n % 5]
            mm(PSB[sbk][:, lo:512], KT[0:96, j * 128:(j + 1) * 128], QT[0:96, i * 512 + lo:(i + 1) * 512],
               True, True, [R_Kd, R_Kx, R_Qd, R_Qx], [R_PS[sbk]])

        def issue_rest(n):
            i, j = tasks[n]
            jj = j - 4 * i
            lo = 128 * jj if jj > 0 else 0
            sbk = SBK[n % 5]
            pt, R_pt = PT[n % NPT], R_PT[n % NPT]
            ob = OB[i % 2]
            act(pt[:, lo:512], PSB[sbk][:, lo:512], AF.Exp, [R_PS[sbk]], [R_pt])
            if jj >= 0:
                tt("pool", pt[:, lo:lo + 128], pt[:, lo:lo + 128], tri, ALU.mult, [R_pt, R_cbf], [R_pt])
            mm(PSB[ob][0:vM, lo:512], Vp[:, j, 0:vM], pt[:, lo:512], j == 0, j == 4 * i + 3,
               [R_Vp, R_pt], [R_PS[ob]])
            if j == 4 * i + 3:
                pend.append((n + 2, lambda: finalize(i)))

        def finalize(i):
            ob = OB[i % 2]
            sl = slice(i * 512, (i + 1) * 512)
            rs, R_rs = M()
            op("dve", lambda e: e.tensor_copy(rs[rsr:rsr + 1, :], PSB[ob][rsr:rsr + 1, :]), [R_PS[ob]], [R_rs])
            mm(PSB[2][:], onesf[rsr:rsr + 1, :], rs[rsr:rsr + 1, :], True, True, [R_onesf, R_rs], [R_PS[2]])
            rc, R_rc = M()
            op("dve", lambda e: e.reciprocal(rc[rows, :], PSB[2][rows, :]), [R_PS[2]], [R_rc], cost=3300.0)
            tO, R_tO = M()
            tt("dve", tO[rows, :], PSB[ob][rows, :], rc[rows, :], ALU.mult, [R_PS[ob], R_rc], [R_tO])
            mg = sqb[i % 2]
            tt("pool", mg[rows, :], tO[rows, :], SG[rows, sl], ALU.mult, [R_tO, R_SG], [R_sqb[i % 2]])
            dma(MG_d[pp, rows, sl], mg[rows, :], reads=[R_sqb[i % 2]], writes=[R_MGd[pp][i]])

        for n in range(len(tasks)):
            issue_S(n)
            issue_rest(n)
            for k, f in list(pend):
                f()
                pend.remove((k, f))

    for pp in range(8):
        fox = pp < 4
        j4 = pp if fox else pp - 4
        cq = (0 if fox else 1544) + 128 * j4
        ck = (512 if fox else 2056) + 128 * j4
        cvv = (1024 if fox else 2568) + 128 * j4
        cg = 3080 + 128 * pp
        for k, c0 in enumerate((cq, ck, cvv, cg)):
            load_w(wb[k], w0_d[:, c0:c0 + 128], V_G0, wst[k % 2], R_wst[k % 2], R_wb[k])
        for tc in range(NCH):
            qk_chunk(wb[0], R_wb[0], tc, 0 if fox else 2, QSq, R_QSq_lo, not fox)
            qk_chunk(wb[1], R_wb[1], tc, 1 if fox else 3, QSk, R_QSk_lo, not fox, load_cs=False)
        dma(QTo[0:64, :], QSq[64:128, :], reads=[R_QSq_lo], writes=[R_QTo_d])
        dma(KTo[0:64, :], QSk[64:128, :], reads=[R_QSk_lo], writes=[R_KTo_d])
        att_args = []
        for odd in (1, 0):
            h = 2 * j4 + odd
            QT, KT = (QTo, KTo) if odd else (QSq, QSk)
            R_Qd, R_Kd = (R_QTo_d, R_KTo_d) if odd else (R_QSq_lo, R_QSk_lo)
            R_Qx, R_Kx = (R_QTo_x, R_KTo_x) if odd else (R_QSq_hi, R_QSk_hi)
            xr = [R_QTo_d, R_KTo_d] if not odd else []
            if fox:
                op("pool", lambda e, QT=QT: e.memset(QT[64:96, :], 0.0), xr, [R_Qx], cols=S // 2)
                op("pool", lambda e, KT=KT: e.memset(KT[64:96, :], 0.0), xr, [R_Kx], cols=S // 2)
                dma(QT[64:67, :], CQ_d[h, :, :], reads=[R_CQd], writes=[R_Qx])
                dma(QT[67:70, :], ones3_d, writes=[R_Qx])
                dma(KT[64:67, :], ones3_d, writes=[R_Kx])
                dma(KT[67:70, :], CK_d[h, :, :], reads=[R_CKd], writes=[R_Kx])
            else:
                dma(KT[64:96, :], blk_d, reads=xr, writes=[R_Kx])
                op("dve", lambda e, KT=KT: e.tensor_reduce(kmf[0:64, 0:NB], KT[0:64, :].rearrange("p (n l) -> p n l", l=256),
                                                            AX.X, ALU.add), [R_Kd], [R_km], cols=S)
                ts("dve", kmb[0:64, 0:NB], kmf[0:64, 0:NB], 1.0 / 256, None, ALU.mult, None, [R_km], [R_km])
                gps = PSB[2][:].rearrange("p (a b) -> p a b", b=16)
                for qt in range(NT):
                    mm(gps[:, qt, :], QT[0:64, qt * 128:(qt + 1) * 128], kmb[0:64, :], True, True,
                       [R_Qd, R_km], [R_PS[2]])
                tt("dve", Gt, gps[:, 0:NT, :], CBt, ALU.add, [R_PS[2], R_CB], [R_G])
                for qt in range(NT):
                    op("dve", lambda e, qt=qt: e.max(top8[:, qt, :], Gt[:, qt, :]), [R_G], [R_top8[qt]], cols=16)
                tt("dve", LTt, Gt, top8[:, :, 2:3].to_broadcast([128, NT, 16]), ALU.is_lt, [R_G] + R_top8, [R_LT])
                tt("dve", mbp[:, :, 0:16], LTt, NEGMt, ALU.mult, [R_LT, R_CB], [R_mbp])
                for g in range(NT // 8):
                    pbk = PSB[3][0:32, :].bitcast(BF16).rearrange("p (a b) -> p a b", b=128)
                    for a in range(8):
                        qt = g * 8 + a
                        op("pe", lambda e, qt=qt, a=a, pbk=pbk: e.transpose(pbk[:, a, :], mbp[:, qt, :], ident),
                           [R_mbp, R_cbf], [R_PS[3]], cols=128)
                    op("act", lambda e, g=g, pbk=pbk: e.copy(out=MBT[:, g * 1024:(g + 1) * 1024].rearrange("p (a b) -> p a b", b=128),
                                                             in_=pbk), [R_PS[3]], [R_MBT], cols=1024)
                dma(QT[64:96, :], MBT, reads=[R_MBT] + xr, writes=[R_Qx])
            att_args.append((pp, QT, R_Qd, R_Qx, KT, R_Kd, R_Kx, odd))
        for g in range(NT // 4):
            b = proj_bank()
            pv = PSB[b][:].rearrange("p (a b) -> p a b", b=128)
            for a in range(4):
                t_ = g * 4 + a
                for c in range(8):
                    mm(pv[:, a, :], hT[:, c, t_ * 128:(t_ + 1) * 128], wb[2][:, c, :], c == 0, c == 7,
                       [R_wb[2], R_hT[t_]], [R_PS[b]])
            op("act", lambda e, pv=pv, g=g: e.copy(out=Vpe[:, g * 4:g * 4 + 4, 0:64], in_=pv[:, :, 0:64]),
               [R_PS[b]], [R_Vp], cols=256)
            op("dve", lambda e, pv=pv, g=g: e.tensor_copy(Vpo[:, g * 4:g * 4 + 4, 64:128], pv[:, :, 64:128]),
               [R_PS[b]], [R_Vp], cols=256)
        for tc in range(NCH):
            b = project_fm(wb[3], R_wb[3], tc)
            act(SG[:, tc * 512:(tc + 1) * 512], PSB[b][:], AF.Silu, [R_PS[b]], [R_SG])
        for a_ in att_args:
            attention(*a_)
    Sc.barrier()
    cv.off = 0
    xt2 = [cv([128, D]) for _ in range(2)]
    x1t = [cv([128, D]) for _ in range(2)]
    R_xt2, R_x1t = [Res(), Res()], [Res(), Res()]
    tmpC = ([cv([128, D], BF16) for _ in range(2)], [cv([128, 1]) for _ in range(2)],
            [cv([128, 1]) for _ in range(2)], [cv([128, D], BF16) for _ in range(2)],
            {k: [Res(), Res()] for k in ("sq", "ss", "rs", "hb")})
    wst2 = [cv([128, 8, 128]) for _ in range(2)]
    R_wst2 = [Res(), Res()]
    wo0b = cv([128, 8, D], BF16)
    R_wo0b = [Res() for _ in range(8)]
    mgl = [cv([128, 8, 512], BF16) for _ in range(2)]
    R_mgl = [Res(), Res()]
    R_X1d = [Res() for _ in range(NT)]
    for n in range(8):
        load_w(wo0b[:, :, n * 128:(n + 1) * 128], wo0_d[:, n * 128:(n + 1) * 128], None, wst2[n % 2], R_wst2[n % 2],
               R_wo0b[n], eng="pool" if n % 2 else "dve")
    for tc in range(NCH):
        m = tc % 2
        mgv = MG_d[:, :, tc * 512:(tc + 1) * 512].rearrange("a p s -> p a s")
        dma(mgl[m][:, 0:4, :], mgv[:, 0:4, :], reads=[R_MGd[pp][tc] for pp in range(8)], writes=[R_mgl[m]])
        dma(mgl[m][:, 4:8, :], mgv[:, 4:8, :], reads=[R_MGd[pp][tc] for pp in range(8)], writes=[R_mgl[m]])
        for tl in range(4):
            ti = tc * 4 + tl
            b2 = ti % 2
            dma(xt2[b2], x_d[ti * 128:(ti + 1) * 128, :], writes=[R_xt2[b2]])
            for n in range(2):
                bk = 4 + (2 * ti + n) % 4
                for pp in range(8):
                    mm(PSB[bk][:], mgl[m][:, pp, tl * 128:(tl + 1) * 128], wo0b[:, pp, n * 512:(n + 1) * 512],
                       pp == 0, pp == 7, [R_mgl[m]] + R_wo0b[n * 4:n * 4 + 4], [R_PS[bk]])
                tt("dve", x1t[b2][:, n * 512:(n + 1) * 512], PSB[bk][:], xt2[b2][:, n * 512:(n + 1) * 512],
                   ALU.add, [R_PS[bk], R_xt2[b2]], [R_x1t[b2]])
            dma(X1_d[ti * 128:(ti + 1) * 128, :], x1t[b2], reads=[R_x1t[b2]], writes=[R_X1d[ti]])
            norm_transpose(ti, x1t[b2], R_x1t[b2], tmpC)
    Sc.barrier()
    if stop_after == "l0":
        for ti in range(NT):
            b2 = ti % 2
            dma(xt2[b2], X1_d[ti * 128:(ti + 1) * 128, :], reads=[R_X1d[ti]], writes=[R_xt2[b2]])
            dma(out_d[ti * 128:(ti + 1) * 128, :], xt2[b2], reads=[R_xt2[b2]])
        return nc, Sc

    cv.off = 0
    CV = cv([128, 8, S], BF16)
    R_CV = [[Res() for _ in range(NCH)] for _ in range(8)]
    NM1 = 7
    misc1 = [cv([128, 512]) for _ in range(NM1)]
    R_misc1 = [Res() for _ in range(NM1)]
    m1 = [0]

    def M1():
        m1[0] = (m1[0] + 1) % NM1
        return misc1[m1[0]], R_misc1[m1[0]]

    rstdT, nmrT = cv([128, 512]), cv([128, 512])
    R_rstdT, R_nmrT = Res(), Res()
    offD = cv.off
    PAD = 32
    UT = cv([128, PAD + S], BF16)
    R_UT = [Res() for _ in range(NCH + 1)]
    DG = [cv([128, 31, 128], BF16) for _ in range(2)]
    R_DG = [Res(), Res()]
    wst3 = [cv([128, 8, 128]) for _ in range(2)]
    R_wst3 = [Res(), Res()]
    wvb, wgb = cv([128, 8, 128], BF16), cv([128, 8, 128], BF16)
    R_wvb, R_wgb = Res(), Res()
    KD = 8
    accs = [cv([128, 512]) for _ in range(2)]
    R_accs = [Res(), Res()]
    op("pool", lambda e: e.memset(UT[:, 0:PAD], 0.0), [], [R_UT[0]])
    for cc in range(8):
        load_w(wvb, w1_d[:, cc * 128:(cc + 1) * 128], V_G1, wst3[0], R_wst3[0], R_wvb)
        load_w(wgb, w1_d[:, 1024 + cc * 128:1024 + (cc + 1) * 128], V_G1, wst3[1], R_wst3[1], R_wgb)
        dg = DG[cc % 2]
        for j in range(KD, 31):
            col = V_CW + cc * 31 + j
            act(dg[:, j, :], ident, AF.Copy, [R_cbf, R_vec], [R_DG[cc % 2]], scale=vec[:, col:col + 1])
        for tc in range(NCH):
            b1 = project_fm(wvb, R_wvb, tc)
            b2 = project_fm(wgb, R_wgb, tc)
            sg, R_sg = M1()
            act(sg, PSB[b2][:], AF.Sigmoid, [R_PS[b2]], [R_sg])
            tt("dve", UT[:, PAD + tc * 512:PAD + (tc + 1) * 512], PSB[b1][:], sg, ALU.mult, [R_PS[b1], R_sg], [R_UT[tc + 1]])
        for tc in range(NCH):
            bk = 5 + tc % 3
            ac, R_ac = accs[tc % 2], R_accs[tc % 2]
            for j in range(KD):
                o0 = tc * 512 + j + PAD - 30
                col = V_CW + cc * 31 + j
                if j == 0:
                    ts("dve", ac, UT[:, o0:o0 + 512], vec[:, col:col + 1], None, ALU.mult, None,
                       [R_UT[tc], R_UT[tc + 1], R_vec], [R_ac])
                else:
                    op("dve", lambda e, o0=o0, col=col, ac=ac: e.scalar_tensor_tensor(
                        ac, UT[:, o0:o0 + 512], vec[:, col:col + 1], ac, ALU.mult, ALU.add),
                       [R_UT[tc], R_UT[tc + 1], R_vec, R_ac], [R_ac], cost=850.0)
            for j in range(KD, 31):
                o0 = tc * 512 + j + PAD - 30
                mm(PSB[bk][:], dg[:, j, :], UT[:, o0:o0 + 512], j == KD, j == 30,
                   [R_DG[cc % 2], R_UT[tc], R_UT[tc + 1]], [R_PS[bk]])
            op("dve", lambda e, bk=bk, tc=tc, ac=ac, cc=cc: e.scalar_tensor_tensor(
                CV[:, cc, tc * 512:(tc + 1) * 512], PSB[bk][:], vec[:, V_CB + cc:V_CB + cc + 1], ac, ALU.add, ALU.add),
               [R_PS[bk], R_vec, R_ac], [R_CV[cc][tc]])
    Sc.barrier()

    cv.off = offD
    w1zb = cv([128, 8, D], BF16)
    wo1b = cv([128, 8, D], BF16)
    R_w1zb = [Res() for _ in range(8)]
    R_wo1b = [Res() for _ in range(8)]
    wst4 = [cv([128, 8, 128]) for _ in range(2)]
    R_wst4 = [Res(), Res()]
    U2 = cv([128, 8, 512], BF16)
    R_U2 = [Res() for _ in range(8)]
    x1l = [cv([128, D]) for _ in range(2)]
    ot = x1l
    R_x1l = [Res(), Res()]
    R_ot = R_x1l
    sqv = [cv([128, 512], BF16) for _ in range(2)]
    R_sqv = [Res(), Res()]
    for n in range(8):
        load_w(w1zb[:, :, n * 128:(n + 1) * 128], w1_d[:, 2048 + n * 128:2048 + (n + 1) * 128], V_G1, wst4[0], R_wst4[0],
               R_w1zb[n], eng="pool")
        load_w(wo1b[:, :, n * 128:(n + 1) * 128], wo1_d[:, n * 128:(n + 1) * 128], None, wst4[1], R_wst4[1],
               R_wo1b[n], eng="dve")
    for tc in range(NCH):
        sl = slice(tc * 512, (tc + 1) * 512)
        for cc in range(8):
            act(sqv[cc % 2], CV[:, cc, sl], AF.Square, [R_CV[cc][tc]], [R_sqv[cc % 2]])
            mm(PSB[2][:], omean, CV[:, cc, sl], cc == 0, cc == 7, [R_cbf, R_CV[cc][tc]], [R_PS[2]])
            mm(PSB[3][:], omean, sqv[cc % 2], cc == 0, cc == 7, [R_cbf, R_sqv[cc % 2]], [R_PS[3]])
        m2, R_m2 = M1()
        act(m2, PSB[2][:], AF.Square, [R_PS[2]], [R_m2])
        var, R_var = M1()
        tt("dve", var, PSB[3][:], m2, ALU.subtract, [R_PS[3], R_m2], [R_var])
        act(var, var, AF.Sqrt, [R_var], [R_var], bias=1e-5, scale=1.0)
        op("dve", lambda e, var=var: e.reciprocal(rstdT, var), [R_var], [R_rstdT], cost=3300.0)
        tt("dve", nmrT, PSB[2][:], rstdT, ALU.mult, [R_PS[2], R_rstdT], [R_nmrT])
        for cc in range(8):
            t1, R_t1 = M1()
            tt("dve", t1, CV[:, cc, sl], rstdT, ALU.mult, [R_CV[cc][tc], R_rstdT], [R_t1])
            t2, R_t2 = M1()
            tt("pool", t2, t1, nmrT, ALU.subtract, [R_t1, R_nmrT], [R_t2])
            a_, R_a = M1()
            act(a_, t2, AF.Silu, [R_t2, R_vec], [R_a], bias=vec[:, V_LB + cc:V_LB + cc + 1],
                scale=vec[:, V_LG + cc:V_LG + cc + 1])
            bz = project_fm(w1zb[:, :, cc * 128:(cc + 1) * 128], R_w1zb[cc], tc)
            sz, R_sz = M1()
            act(sz, PSB[bz][:], AF.Silu, [R_PS[bz]], [R_sz])
            tt("pool", U2[:, cc, :], a_, sz, ALU.mult, [R_a, R_sz], [R_U2[cc]])
        for tl in range(4):
            ti = tc * 4 + tl
            b2 = ti % 2
            dma(x1l[b2], X1_d[ti * 128:(ti + 1) * 128, :], reads=[R_X1d[ti]], writes=[R_x1l[b2]])
            for n in range(2):
                bk = 5 + (2 * ti + n) % 3
                for cc in range(8):
                    mm(PSB[bk][:], U2[:, cc, tl * 128:(tl + 1) * 128], wo1b[:, cc, n * 512:(n + 1) * 512],
                       cc == 0, cc == 7, [R_U2[cc]] + R_wo1b[n * 4:n * 4 + 4], [R_PS[bk]])
                tt("dve", ot[b2][:, n * 512:(n + 1) * 512], PSB[bk][:], x1l[b2][:, n * 512:(n + 1) * 512],
                   ALU.add, [R_PS[bk], R_x1l[b2]], [R_ot[b2]])
            dma(out_d[ti * 128:(ti + 1) * 128, :], ot[b2], reads=[R_ot[b2]])
    return nc, Sc


_CACHE = {}


def _prep_inputs(inp, S):
    c = host_consts(S)
    shared = dict(c)
    shared["w_in0"] = np.ascontiguousarray(inp["l0_w_in"], np.float32)
    shared["w_out0"] = np.ascontiguousarray(inp["l0_w_out"], np.float32)
    shared["w_in1"] = np.ascontiguousarray(inp["l1_w_in"], np.float32)
    shared["w_out1"] = np.ascontiguousarray(inp["l1_w_out"], np.float32)
    shared["vec"] = host_vec(inp)
    return shared


def run(inp, S, ncores, stop_after=None, trace=False):
    nc, Sc = build(S, stop_after)
    with nc.Block() as block:
        import os as _os
        Sc.emit(block, reorder=(_os.environ.get('NOREORDER') is None))
    shared = _prep_inputs(inp, S)
    x = np.asarray(inp["x"], np.float32)
    in_maps = [dict(shared, x=np.ascontiguousarray(x[b])) for b in range(ncores)]
    res = run_bass_kernel_spmd(nc, in_maps, core_ids=list(range(ncores)), trace=trace)
    out = np.stack([np.asarray(r["out"], np.float32) for r in res.results], axis=0)
    return out, res


def kernel(**inputs):
    out, _ = run(inputs, 4096, NCORES)
    return out
```

```python
import numpy as np
import ml_dtypes
import concourse.bass as bass
import concourse.mybir as mybir
from concourse.bass_utils import run_bass_kernel_spmd

F32 = mybir.dt.float32
BF16 = mybir.dt.bfloat16
AF = mybir.ActivationFunctionType
ALU = mybir.AluOpType
AX = mybir.AxisListType

NCORES = 8
D = 1024
HD = 64
NEG = -30000.0


DEBUG_LINES = None


class Res:
    __slots__ = ("w", "r", "name", "const")

    def __init__(self, name="", const=False):
        self.w = None
        self.r = []
        self.name = name
        self.const = const


class Op:
    __slots__ = ("eng", "fn", "dma", "deps", "cost", "idx", "cnt", "dsem", "dval", "prev_slot",
                 "signal", "start", "finish", "nbytes", "waits", "ndesc", "line")

    def __init__(self, eng, fn, dma, cost, nbytes=0):
        self.eng = eng
        self.fn = fn
        self.dma = dma
        self.deps = {}
        self.cost = cost
        self.nbytes = nbytes
        self.signal = False
        self.cnt = None
        self.dsem = None
        self.dval = None
        self.prev_slot = None
        self.start = None
        self.finish = None
        self.waits = ()
        self.ndesc = 128


class Sched:
    ENGS = ("pe", "act", "dve", "pool", "sp")
    CE = ("pe", "act", "dve", "pool")
    NDMA = 12
    WINDOW = 192
    LAT = 300.0
    MAXDESC = 1024

    def __init__(self, nc):
        self.nc = nc
        self.ops = {e: [] for e in self.ENGS}
        self.sem = {e: nc.alloc_semaphore("s_" + e) for e in self.CE}
        self.dsems = {e: [nc.alloc_semaphore("d_%s%d" % (e, i)) for i in range(self.NDMA)]
                      for e in ("sp", "pool", "act")}
        self.nops = 0
        self.since_bar = []
        self.bar = {}

    @staticmethod
    def _need_wait(d, o, kind):
        if d.dma or o.dma or d.eng != o.eng:
            return True
        return o.eng != "pe"

    def _add(self, o):
        o.idx = self.nops
        self.nops += 1
        b = self.bar.get(o.eng)
        if b is not None and id(b) not in o.deps:
            o.deps[id(b)] = (b, False)
        self.ops[o.eng].append(o)
        self.since_bar.append(o)

    def op(self, eng, fn, reads=(), writes=(), dma=False, cost=300.0, nbytes=0):
        o = Op(eng, fn, dma, cost, nbytes)
        if DEBUG_LINES is not None:
            import sys as _sys
            f = _sys._getframe(1)
            chain = []
            while f is not None and len(chain) < 4:
                chain.append(f.f_lineno)
                f = f.f_back
            o.line = chain

        def dep(d, kind):
            if d is o:
                return
            nw = self._need_wait(d, o, kind)
            old = o.deps.get(id(d))
            o.deps[id(d)] = (d, nw or (old[1] if old else False))

        for r in reads:
            if r.w is not None:
                dep(r.w, "raw")
        for w in writes:
            if w.w is not None:
                dep(w.w, "raw")
            for rd in w.r:
                dep(rd, "war")
        for r in reads:
            if not r.const:
                r.r.append(o)
        for w in writes:
            w.w = o
            w.r = []
        self._add(o)
        return o

    def barrier(self):
        prior = list(self.since_bar)
        self.since_bar = []
        newbar = {}
        for e in self.ENGS:
            o = Op(e, lambda eng: eng.nop(), False, 50.0)
            for d in prior:
                o.deps[id(d)] = (d, d.dma or d.eng != e)
            newbar[e] = o
        for e in self.ENGS:
            self.bar[e] = None
            self._add(newbar[e])
        self.bar = newbar
        self.since_bar = []

    def schedule(self):
        ENGS = self.ENGS
        LAT = self.LAT
        users = {}
        for e in ENGS:
            for o in self.ops[e]:
                o.start = None
                o.finish = None
                o.cnt = None
                o.signal = False
                o.waits = [len(o.deps), 0.0]
                for d, nw in o.deps.values():
                    users.setdefault(id(d), []).append((o, nw))
        unsched = {e: list(self.ops[e]) for e in ENGS}
        pos = {e: 0 for e in ENGS}
        free = {e: 0.0 for e in ENGS}
        order = {e: [] for e in ENGS}
        bus = 0.0
        total = sum(len(v) for v in unsched.values())
        done = 0
        W = self.WINDOW
        while done < total:
            best = None
            for e in ENGS:
                lst = unsched[e]
                p = pos[e]
                n = len(lst)
                while p < n and lst[p] is None:
                    p += 1
                pos[e] = p
                seen = 0
                q = p
                fe = free[e]
                while q < n and seen < W:
                    o = lst[q]
                    q += 1
                    if o is None:
                        continue
                    seen += 1
                    w = o.waits
                    if w[0]:
                        continue
                    st = w[1] if w[1] > fe else fe
                    if best is None or (st, o.idx) < best[0]:
                        best = ((st, o.idx), e, q - 1, o)
                    if st <= fe:
                        break
            key, e, q, o = best
            st = key[0]
            o.start = st
            if o.dma:
                t0 = st if st > bus else bus
                bus = t0 + o.nbytes / 220.0
                o.finish = bus + 2000.0
                free[e] = st + 60.0
            else:
                o.finish = st + o.cost
                free[e] = o.finish
            for u, nw in users.get(id(o), ()):
                uw = u.waits
                uw[0] -= 1
                if nw:
                    t = o.finish + (LAT if (o.eng != u.eng or o.dma) else 0.0)
                else:
                    t = o.start
                if t > uw[1]:
                    uw[1] = t
            unsched[e][q] = None
            order[e].append(o)
            done += 1
        self.makespan = max(free.values())
        return order

    def emit(self, block, reorder=True):
        order = self.schedule() if reorder else {e: list(self.ops[e]) for e in self.ENGS}
        qpos = {}
        for e in self.ENGS:
            for i, o in enumerate(order[e]):
                qpos[id(o)] = i
        all_dma = []
        for e in self.ENGS:
            k = 0
            last = [None] * self.NDMA
            val = [0] * self.NDMA
            for o in order[e]:
                if not o.dma:
                    continue
                slot = k % self.NDMA
                k += 1
                o.prev_slot = last[slot]
                last[slot] = o
                val[slot] += 16
                o.dsem = self.dsems[e][slot]
                o.dval = val[slot]
                all_dma.append(o)
        for e in self.ENGS:
            fifo, fifo_n = [], [0]
            for o in order[e]:
                latest = {}
                dl = []
                for d, nw in o.deps.values():
                    if not nw:
                        continue
                    if d.dma:
                        dl.append(d)
                    else:
                        c = latest.get(d.eng)
                        if c is None or qpos[id(d)] > qpos[id(c)]:
                            latest[d.eng] = d
                if o.dma and o.prev_slot is not None:
                    dl.append(o.prev_slot)
                if o.dma:
                    while fifo and fifo_n[0] + o.ndesc > self.MAXDESC:
                        old = fifo.pop(0)
                        fifo_n[0] -= old.ndesc
                        dl.append(old)
                    fifo.append(o)
                    fifo_n[0] += o.ndesc
                dmax = {}
                for d in dl:
                    c = dmax.get(id(d.dsem))
                    if c is None or d.dval > c.dval:
                        dmax[id(d.dsem)] = d
                o.waits = list(latest.values()) + list(dmax.values())
                for d in latest.values():
                    d.signal = True
        tails = []
        for ce in self.CE:
            comp = [o for o in order[ce] if not o.dma]
            if comp:
                comp[-1].signal = True
                tails.append(comp[-1])
        for e in self.CE:
            c = 0
            for o in order[e]:
                if not o.dma and o.signal:
                    c += 1
                    o.cnt = c

        def run(e, engobj):
            waited = {}

            def wait(d):
                if d.dma:
                    key, val, sem = ("d", id(d.dsem)), d.dval, d.dsem
                else:
                    key, val, sem = ("c", d.eng), d.cnt, self.sem[d.eng]
                if waited.get(key, 0) >= val:
                    return
                waited[key] = val
                engobj.wait_ge(sem, val)

            for o in order[e]:
                for d in o.waits:
                    wait(d)
                ins = o.fn(engobj)
                if DEBUG_LINES is not None:
                    try:
                        DEBUG_LINES[str(ins.ins.name)] = o.line
                    except Exception:
                        pass
                if o.dma:
                    ins.then_inc(o.dsem, 16)
                elif o.signal:
                    ins.then_inc(self.sem[e], 1)
            if e == "sp":
                for d in all_dma:
                    wait(d)
                for d in tails:
                    wait(d)

        @block.tensor
        def _(eng):
            run("pe", eng)

        @block.scalar
        def _(eng):
            run("act", eng)

        @block.vector
        def _(eng):
            run("dve", eng)

        @block.gpsimd
        def _(eng):
            run("pool", eng)

        @block.sync
        def _(eng):
            run("sp", eng)


def host_consts(S):
    NT = S // 128
    bf = ml_dtypes.bfloat16
    ident = np.eye(128, dtype=np.float32)
    tri = (np.arange(128)[:, None] <= np.arange(128)[None, :]).astype(np.float32)
    rot = np.zeros((64, 64), np.float32)
    for d in range(8):
        rot[d + 8, d] = -1.0
        rot[d, d + 8] = 1.0
    rot2 = np.zeros((128, 128), np.float32)
    rot2[:64, :64] = rot
    rot2[64:, 64:] = rot
    bo = np.zeros((128, 128), np.float32)
    bo[:64, :64] = 1.0 / 64
    bo[64:, 64:] = 1.0 / 64
    om = np.full((128, 128), 1.0 / 1024, np.float32)
    cbf = np.concatenate([ident, tri, rot2, bo, om], axis=1).astype(bf)
    half = 8
    inv_freq = (500000.0 ** (-np.arange(half, dtype=np.float32) / half)).astype(np.float32)
    ang = np.arange(S, dtype=np.float32)[:, None] * inv_freq[None, :]
    cos = np.cos(ang).astype(np.float32)
    sin = np.sin(ang).astype(np.float32)
    cosT = np.ones((128, S), np.float32)
    sinT = np.zeros((128, S), np.float32)
    for base in (0, 64):
        cosT[base:base + 8] = cos.T
        cosT[base + 8:base + 16] = cos.T
        sinT[base:base + 8] = sin.T
        sinT[base + 8:base + 16] = sin.T
    blk = np.zeros((32, S), np.float32)
    blk[(np.arange(S) // 256), np.arange(S)] = 1.0
    cb = np.zeros((128, NT, 16), np.float32)
    negm = np.full((128, NT, 16), NEG, np.float32)
    for qt in range(NT):
        own = qt // 2
        cb[:, qt, own:] = -1e30
        negm[:, qt, own] = 0.0
    ones3 = np.ones((3, S), np.float32).astype(bf)
    return dict(cbf=cbf, cosT=cosT, sinT=sinT, blk1h=blk.astype(bf),
                cbm=cb.reshape(128, NT * 16), negm=negm.reshape(128, NT * 16), ones3=ones3,
                onesf=np.ones((128, 128), np.float32))


def host_vec(inp):
    def c8(v):
        return np.asarray(v, np.float32).reshape(8, 128).T
    def t2(v):
        return np.tile(np.asarray(v, np.float32), 2).reshape(128, 1)
    bfv = np.zeros((128, 1), np.float32)
    bfv[:8, 0] = np.asarray(inp["l0_b_f"], np.float32)
    cw = np.asarray(inp["l1_conv_w"], np.float32).reshape(31, 8, 128).transpose(2, 1, 0).reshape(128, 248)
    cols = [c8(inp["l0_norm"]), c8(inp["l1_norm"]), t2(inp["l0_qn_fox"]), t2(inp["l0_kn_fox"]),
            t2(inp["l0_qn_moba"]), t2(inp["l0_kn_moba"]), c8(inp["l1_conv_b"]), c8(inp["l1_ln_g"]),
            c8(inp["l1_ln_b"]), bfv, cw]
    return np.ascontiguousarray(np.concatenate(cols, axis=1))
V_G0, V_G1, V_GAIN, V_CB, V_LG, V_LB, V_BF, V_CW, NV = 0, 8, 16, 20, 28, 36, 44, 45, 45 + 248


def build(S, stop_after=None):
    NT = S // 128
    NCH = S // 512
    NB = S // 256
    nc = bass.Bass("TRN2", target_bir_lowering=False)
    di = lambda n, sh, dt=F32: nc.dram_tensor(n, sh, dt, kind="ExternalInput").ap()
    x_d = di("x", [S, D])
    w0_d = di("w_in0", [D, 4104])
    wo0_d = di("w_out0", [D, D])
    w1_d = di("w_in1", [D, 3072])
    wo1_d = di("w_out1", [D, D])
    vec_d = di("vec", [128, NV])
    cbf_d = di("cbf", [128, 640], BF16)
    cos_d = di("cosT", [128, S])
    sin_d = di("sinT", [128, S])
    blk_d = di("blk1h", [32, S], BF16)
    cbm_d = di("cbm", [128, NT * 16])
    negm_d = di("negm", [128, NT * 16])
    ones3_d = di("ones3", [3, S], BF16)
    onesf_d = di("onesf", [128, 128])
    out_d = nc.dram_tensor("out", [S, D], F32, kind="ExternalOutput").ap()
    X1_d = nc.dram_tensor("x1s", [S, D], F32, kind="Internal").ap()
    MG_d = nc.dram_tensor("mgs", [8, 128, S], BF16, kind="Internal").ap()
    CK_d = nc.dram_tensor("cks", [8, 3, S], BF16, kind="Internal").ap()
    CQ_d = nc.dram_tensor("cqs", [8, 3, S], BF16, kind="Internal").ap()

    Sc = Sched(nc)

    sb = lambda n, sh, dt=F32: nc.alloc_sbuf_tensor(n, sh, dt)
    vec = sb("vecs", [128, NV])
    cbf = sb("cbfs", [128, 640], BF16)
    onesf = sb("onesfs", [128, 128])
    gsc = sb("gsc", [128, 4])
    nbf = sb("nbf", [128, 1])
    hT = sb("hT", [128, 8, S], BF16)
    ident, tri, rot2, bones, omean = (cbf[:, i * 128:(i + 1) * 128] for i in range(5))
    ARB = (nc.sbuf_bytes_remaining - 512) // 128 * 128
    arena = sb("arena", [128, ARB // 2], BF16)
    R_vec, R_cbf, R_onesf, R_gsc, R_nbf = Res(), Res(), Res(), Res(), Res()
    R_hT = [Res() for _ in range(NT)]

    class Carver:
        def __init__(self):
            self.off = 0

        def __call__(self, shape, dt=F32):
            esz = 4 if dt == F32 else 2
            n = int(np.prod(shape[1:])) * esz
            n = (n + 63) // 64 * 64
            a = arena[:, self.off // 2:(self.off + n) // 2]
            self.off += n
            assert self.off <= ARB, self.off
            v = a.bitcast(F32) if dt == F32 else a
            nfree = int(np.prod(shape[1:]))
            v = v[0:shape[0], 0:nfree]
            if len(shape) == 3:
                v = v.rearrange("p (a b) -> p a b", b=shape[2])
            return v

    PSB = [nc.alloc_psum_tensor("psb%d" % i, [128, 512], F32) for i in range(8)]
    R_PS = [Res() for _ in range(8)]

    def ncols(ap):
        return int(np.prod(ap.shape[1:]))

    def op(eng, fn, reads=(), writes=(), dma=False, cols=512, cost=None, nbytes=0):
        if cost is None:
            cost = {"pe": 16 + cols / 2.4, "act": 220 + cols / 1.4, "dve": 60 + cols * 1.3,
                    "pool": 100 + cols * 2.1, "sp": 60.0}[eng]
        return Sc.op(eng, fn, reads, writes, dma=dma, cost=cost, nbytes=nbytes)

    def dma(out, in_, reads=(), writes=(), eng="sp", **kw):
        nb = int(np.prod(out.shape)) * (4 if out.dtype == F32 else 2)
        o = op(eng, lambda e: e.dma_start(out=out, in_=in_, **kw), reads, writes, dma=True, nbytes=nb)
        o.ndesc = max(int(np.prod(out.shape[:-1])), int(np.prod(in_.shape[:-1])))
        return o

    def mm(out, lhsT, rhs, start, stop, reads, writes):
        n = ncols(rhs) * (4 if rhs.dtype == F32 else 1)
        return op("pe", lambda e: e.matmul(out, lhsT, rhs, start=start, stop=stop), reads, writes, cols=n)

    def act(out, in_, func, reads, writes, **kw):
        return op("act", lambda e: e.activation(out=out, in_=in_, func=func, **kw), reads, writes, cols=ncols(out))

    def tt(eng, out, a, b, alu, reads, writes):
        return op(eng, lambda e: e.tensor_tensor(out, a, b, alu), reads, writes, cols=ncols(out))

    def ts(eng, out, a, s1, s2, o0, o1, reads, writes):
        if o1 is None:
            return op(eng, lambda e: e.tensor_scalar(out, a, s1, None, o0), reads, writes, cols=ncols(out))
        return op(eng, lambda e: e.tensor_scalar(out, a, s1, s2, o0, o1), reads, writes, cols=ncols(out))

    dma(vec[:], vec_d, writes=[R_vec])
    dma(cbf[:], cbf_d, writes=[R_cbf])
    dma(onesf[:], onesf_d, writes=[R_onesf])
    ts("dve", gsc[:, 0:1], vec[:, V_GAIN:V_GAIN + 1], 0.125, None, ALU.mult, None, [R_vec], [R_gsc])
    op("dve", lambda e: e.tensor_copy(gsc[:, 1:2], vec[:, V_GAIN + 1:V_GAIN + 2]), [R_vec], [R_gsc], cols=1)
    ts("dve", gsc[:, 2:3], vec[:, V_GAIN + 2:V_GAIN + 3], 0.125, None, ALU.mult, None, [R_vec], [R_gsc])
    op("dve", lambda e: e.tensor_copy(gsc[:, 3:4], vec[:, V_GAIN + 3:V_GAIN + 4]), [R_vec], [R_gsc], cols=1)
    ts("dve", nbf[:], vec[:, V_BF:V_BF + 1], -1.0, None, ALU.mult, None, [R_vec], [R_nbf])

    def norm_transpose(ti, xt, R_x, tmp, nset=2, banks=(0, 1)):
        sq, ss, rs, hb, R = tmp
        b = ti % nset
        bkk = banks[ti % len(banks)]
        act(sq[b], xt, AF.Square, [R_x], [R["sq"][b], R["ss"][b]], accum_out=ss[b])
        act(rs[b], ss[b], AF.Sqrt, [R["ss"][b]], [R["rs"][b]], bias=1e-6, scale=1.0 / D)
        op("dve", lambda e: e.reciprocal(rs[b], rs[b]), [R["rs"][b]], [R["rs"][b]], cols=1)
        ts("dve", hb[b], xt, rs[b], None, ALU.mult, None, [R_x, R["rs"][b]], [R["hb"][b]])
        pb = PSB[bkk][:].bitcast(BF16).rearrange("p (a b) -> p a b", b=128)
        for c in range(8):
            op("pe", lambda e, c=c: e.transpose(pb[:, c, :], hb[b][:, c * 128:(c + 1) * 128], ident),
               [R["hb"][b], R_cbf], [R_PS[bkk]], cols=128)
        op("act" if ti % 2 else "dve",
           (lambda e: e.copy(out=hT[:, :, ti * 128:(ti + 1) * 128], in_=pb)) if ti % 2 else
           (lambda e: e.tensor_copy(hT[:, :, ti * 128:(ti + 1) * 128], pb)),
           [R_PS[bkk]], [R_hT[ti]], cols=1024)

    def load_w(dst_bf, src_ap, gcol, stage, R_stage, R_dst, eng="pool"):
        n = dst_bf.shape[2]
        srcv = src_ap.rearrange("(c p) n -> p c n", p=128)
        dma(stage[:, 0:4, 0:n], srcv[:, 0:4, :], writes=[R_stage])
        dma(stage[:, 4:8, 0:n], srcv[:, 4:8, :], writes=[R_stage])
        if gcol is None:
            op(eng, lambda e: e.tensor_copy(dst_bf, stage[:, :, 0:n]), [R_stage], [R_dst])
        else:
            g = vec[:, gcol:gcol + 8].unsqueeze(2).to_broadcast([128, 8, n])
            tt(eng, dst_bf, stage[:, :, 0:n], g, ALU.mult, [R_stage, R_vec], [R_dst])

    cv = Carver()
    xt_ = [cv([128, D]) for _ in range(2)]
    R_xt = [Res(), Res()]
    tmpA = ([cv([128, D], BF16) for _ in range(2)], [cv([128, 1]) for _ in range(2)],
            [cv([128, 1]) for _ in range(2)], [cv([128, D], BF16) for _ in range(2)],
            {k: [Res(), Res()] for k in ("sq", "ss", "rs", "hb")})
    NSA = 4 if S >= 4096 else 2
    if NSA == 4:
        tmpA[1].extend([cv([128, 1]) for _ in range(2)])
        tmpA[2].extend([cv([128, 1]) for _ in range(2)])
        for k_ in ("sq", "ss", "rs", "hb"):
            tmpA[4][k_].extend([Res(), Res()])
        R_xt.extend([Res(), Res()])
    off_common = cv.off
    wst = [cv([128, 8, 128]) for _ in range(2)]
    R_wst = [Res(), Res()]
    wb = [cv([128, 8, 128], BF16) for _ in range(4)]
    R_wb = [Res() for _ in range(4)]
    wfb = cv([128, 8, 8], BF16)
    R_wfb = Res()
    NPT = 5
    PT = [cv([128, 512], BF16) for _ in range(NPT)]
    R_PT = [Res() for _ in range(NPT)]
    NM = 8
    misc = [cv([128, 512]) for _ in range(NM)]
    R_misc = [Res() for _ in range(NM)]
    sqb = [cv([128, 512], BF16) for _ in range(3)]
    R_sqb = [Res() for _ in range(3)]
    qnbs = [cv([128, 512], BF16) for _ in range(3)]
    R_qnbs = [Res() for _ in range(3)]
    cs_t = [(cv([128, 512]), cv([128, 512])) for _ in range(2)]
    R_cs = [Res(), Res()]
    Vpe = cv([128, NT, 65], BF16)
    Vpo = cv([128, NT, 128], BF16)
    R_Vp = Res()
    CBt = cv([128, NT, 16])
    NEGMt = cv([128, NT, 16])
    Gt = cv([128, NT, 16])
    LTt = cv([128, NT, 16])
    top8 = cv([128, NT, 8])
    mbp = cv([128, NT, 32], BF16)
    kmf = cv([128, 16])
    kmb = cv([128, 16], BF16)
    R_CB, R_G, R_LT, R_mbp, R_km = Res(), Res(), Res(), Res(), Res()
    R_top8 = [Res() for _ in range(NT)]
    off_R1 = cv.off
    E_t = cv([8, S])
    off_CN = cv.off
    CN_t = cv([8, S])
    TA_t = cv([8, S], BF16)
    TB_t = cv([8, S], BF16)
    R_E, R_CN, R_TA, R_TB = Res(), Res(), Res(), Res()
    R_CKd, R_CQd = Res(), Res()
    if NSA == 4:
        cv2 = Carver()
        cv2.off = off_CN
        for _ in range(2):
            xt_.append(cv2([128, D]))
            tmpA[0].append(cv2([128, D], BF16))
            tmpA[3].append(cv2([128, D], BF16))
        assert cv2.off <= off_CN + 8 * S * 4 // 8 * 8 and cv2.off - off_CN <= 4 * S
    for ti in range(NT):
        b = ti % NSA
        dma(xt_[b], x_d[ti * 128:(ti + 1) * 128, :], writes=[R_xt[b]])
        norm_transpose(ti, xt_[b], R_xt[b], tmpA, nset=NSA, banks=(0, 1, 4, 5))

    dma(CBt.rearrange("p a b -> p (a b)"), cbm_d, writes=[R_CB])
    dma(NEGMt.rearrange("p a b -> p (a b)"), negm_d, writes=[R_CB])
    op("pool", lambda e: e.memset(Vpe[:, :, 64:65], 1.0), [], [R_Vp])
    op("pool", lambda e: e.memset(Vpo[:, :, 0:64], 0.0), [], [R_Vp])
    op("pool", lambda e: e.memset(Vpo[:, :, 0:1], 1.0), [], [R_Vp])
    op("pool", lambda e: e.memset(mbp[:], 0.0), [], [R_mbp])
    op("pool", lambda e: e.memset(kmb[:], 0.0), [], [R_km])

    wfv = w0_d[:, 1536:1544].rearrange("(c p) n -> p c n", p=128)
    dma(wst[0][:, 0:4, 0:8], wfv[:, 0:4, :], writes=[R_wst[0]])
    dma(wst[0][:, 4:8, 0:8], wfv[:, 4:8, :], writes=[R_wst[0]])
    tt("pool", wfb, wst[0][:, :, 0:8], vec[:, V_G0:V_G0 + 8].unsqueeze(2).to_broadcast([128, 8, 8]), ALU.mult,
       [R_wst[0], R_vec], [R_wfb])
    for tc in range(NCH):
        b = tc % 2
        for c in range(8):
            mm(PSB[b][0:8, :], wfb[:, c, :], hT[:, c, tc * 512:(tc + 1) * 512], c == 0, c == 7,
               [R_wfb] + R_hT[tc * 4:tc * 4 + 4], [R_PS[b]])
        act(E_t[:, tc * 512:(tc + 1) * 512], PSB[b][0:8, :], AF.Exp, [R_PS[b], R_nbf], [R_E],
            bias=nbf[0:8, :], scale=-1.0)
    act(E_t, E_t, AF.Ln, [R_E], [R_E], bias=1.0, scale=1.0)
    op("pool", lambda e: e.memset(TA_t, 1.0), [], [R_TA], cols=S)
    op("dve", lambda e: e.tensor_tensor_scan(CN_t, TA_t, E_t, 0.0, ALU.mult, ALU.add), [R_TA, R_E], [R_CN], cols=2 * S)
    srcs = [(CN_t, R_CN, E_t, R_E), (E_t, R_E, CN_t, R_CN), (CN_t, R_CN, None, None)]
    for p, (src, R_src, dst, R_dst) in enumerate(srcs):
        op("dve", lambda e, src=src: e.tensor_copy(TA_t, src), [R_src], [R_TA], cols=S)
        dma(CK_d[:, p, :], TA_t, reads=[R_TA], writes=[R_CKd])
        ts("dve", TB_t, TA_t, -1.0, None, ALU.mult, None, [R_TA], [R_TB])
        dma(CQ_d[:, p, :], TB_t, reads=[R_TB], writes=[R_CQd])
        if dst is not None:
            tt("dve", dst, src, TA_t, ALU.subtract, [R_src, R_TA], [R_dst])
    Sc.barrier()

    cv.off = off_R1
    QSq = cv([128, S], BF16)
    QSk = cv([128, S], BF16)
    QTo = cv([128, S], BF16)
    KTo = cv([128, S], BF16)
    SG = cv([128, S], BF16)
    MBT = cv([32, S], BF16)
    R_QSq_lo, R_QSq_hi, R_QSk_lo, R_QSk_hi = Res(), Res(), Res(), Res()
    R_QTo_d, R_QTo_x, R_KTo_d, R_KTo_x, R_SG, R_MBT = Res(), Res(), Res(), Res(), Res(), Res()
    R_MGd = [[Res() for _ in range(NCH)] for _ in range(8)]

    mi = [0]

    def M():
        mi[0] = (mi[0] + 1) % NM
        return misc[mi[0]], R_misc[mi[0]]

    pj = [0]
    PJB = [0, 1, 4]

    def proj_bank():
        pj[0] = (pj[0] + 1) % 3
        return PJB[pj[0]]

    def project_fm(wt, R_w, tc):
        b = proj_bank()
        for c in range(8):
            mm(PSB[b][:], wt[:, c, :], hT[:, c, tc * 512:(tc + 1) * 512], c == 0, c == 7,
               [R_w] + R_hT[tc * 4:tc * 4 + 4], [R_PS[b]])
        return b

    uq = [0]

    def qk_chunk(wt, R_w, tc, gcol, dst, R_dst, rope, load_cs=True):
        b = project_fm(wt, R_w, tc)
        uq[0] += 1
        u = uq[0]
        s = u % 3
        sb2 = (2, 5)[u % 2]
        sb3 = (3, 6)[u % 2]
        act(sqb[s], PSB[b][:], AF.Square, [R_PS[b]], [R_sqb[s]])
        mm(PSB[sb2][:], bones, sqb[s], True, True, [R_cbf, R_sqb[s]], [R_PS[sb2]])
        sd, R_sd = M()
        act(sd, PSB[sb2][:], AF.Ln, [R_PS[sb2]], [R_sd], bias=1e-6, scale=1.0)
        act(sd, sd, AF.Exp, [R_sd], [R_sd], scale=-0.5)
        sl = slice(tc * 512, (tc + 1) * 512)
        if not rope:
            op("dve", lambda e: e.scalar_tensor_tensor(dst[:, sl], PSB[b][:], gsc[:, gcol:gcol + 1], sd,
                                                       ALU.mult, ALU.mult), [R_PS[b], R_gsc, R_sd], [R_dst])
            return
        qn, R_qn = M()
        op("dve", lambda e: e.scalar_tensor_tensor(qn, PSB[b][:], gsc[:, gcol:gcol + 1], sd,
                                                   ALU.mult, ALU.mult), [R_PS[b], R_gsc, R_sd], [R_qn])
        qnb = qnbs[s]
        op("act", lambda e: e.copy(out=qnb, in_=qn), [R_qn], [R_qnbs[s]])
        mm(PSB[sb3][:], rot2, qnb, True, True, [R_cbf, R_qnbs[s]], [R_PS[sb3]])
        c2 = tc % 2
        ct, st = cs_t[c2]
        if load_cs:
            dma(ct, cos_d[:, sl], writes=[R_cs[c2]])
            dma(st, sin_d[:, sl], writes=[R_cs[c2]])
        t1, R_t1 = M()
        tt("dve", t1, qn, ct, ALU.mult, [R_qn, R_cs[c2]], [R_t1])
        t2, R_t2 = M()
        tt("dve", t2, PSB[sb3][:], st, ALU.mult, [R_PS[sb3], R_cs[c2]], [R_t2])
        tt("pool", dst[:, sl], t1, t2, ALU.add, [R_t1, R_t2], [R_dst])

    OB = [7, 3]
    SBK = [4, 5, 6, 0, 1]

    def attention(pp, QT, R_Qd, R_Qx, KT, R_Kd, R_Kx, odd):
        Vp = Vpo if odd else Vpe
        vM = 128 if odd else 65
        rows = slice(64, 128) if odd else slice(0, 64)
        rsr = 0 if odd else 64
        tasks = [(i, j) for i in range(NCH) for j in range(4 * i + 4)]
        pend = []

        def issue_S(n):
            i, j = tasks[n]
            jj = j - 4 * i
            lo = 128 * jj if jj > 0 else 0
            sbk = SBK[n % 5]
            mm(PSB[sbk][:, lo:512], KT[0:96, j * 128:(j + 1) * 128], QT[0:96, i * 512 + lo:(i + 1) * 512],
               True, True, [R_Kd, R_Kx, R_Qd, R_Qx], [R_PS[sbk]])

        def issue_rest(n):
            i, j = tasks[n]
            jj = j - 4 * i
            lo = 128 * jj if jj > 0 else 0
            sbk = SBK[n % 5]
            pt, R_pt = PT[n % NPT], R_PT[n % NPT]
            ob = OB[i % 2]
            act(pt[:, lo:512], PSB[sbk][:, lo:512], AF.Exp, [R_PS[sbk]], [R_pt])
            if jj >= 0:
                tt("pool", pt[:, lo:lo + 128], pt[:, lo:lo + 128], tri, ALU.mult, [R_pt, R_cbf], [R_pt])
            mm(PSB[ob][0:vM, lo:512], Vp[:, j, 0:vM], pt[:, lo:512], j == 0, j == 4 * i + 3,
               [R_Vp, R_pt], [R_PS[ob]])
            if j == 4 * i + 3:
                pend.append((n + 2, lambda: finalize(i)))

        def finalize(i):
            ob = OB[i % 2]
            sl = slice(i * 512, (i + 1) * 512)
            rs, R_rs = M()
            op("dve", lambda e: e.tensor_copy(rs[rsr:rsr + 1, :], PSB[ob][rsr:rsr + 1, :]), [R_PS[ob]], [R_rs])
            mm(PSB[2][:], onesf[rsr:rsr + 1, :], rs[rsr:rsr + 1, :], True, True, [R_onesf, R_rs], [R_PS[2]])
            rc, R_rc = M()
            op("dve", lambda e: e.reciprocal(rc[rows, :], PSB[2][rows, :]), [R_PS[2]], [R_rc], cost=3300.0)
            tO, R_tO = M()
            tt("dve", tO[rows, :], PSB[ob][rows, :], rc[rows, :], ALU.mult, [R_PS[ob], R_rc], [R_tO])
            mg = sqb[i % 2]
            tt("pool", mg[rows, :], tO[rows, :], SG[rows, sl], ALU.mult, [R_tO, R_SG], [R_sqb[i % 2]])
            dma(MG_d[pp, rows, sl], mg[rows, :], reads=[R_sqb[i % 2]], writes=[R_MGd[pp][i]])

        for n in range(len(tasks)):
            issue_S(n)
            issue_rest(n)
            for k, f in list(pend):
                f()
                pend.remove((k, f))

    for pp in range(8):
        fox = pp < 4
        j4 = pp if fox else pp - 4
        cq = (0 if fox else 1544) + 128 * j4
        ck = (512 if fox else 2056) + 128 * j4
        cvv = (1024 if fox else 2568) + 128 * j4
        cg = 3080 + 128 * pp
        for k, c0 in enumerate((cq, ck, cvv, cg)):
            load_w(wb[k], w0_d[:, c0:c0 + 128], V_G0, wst[k % 2], R_wst[k % 2], R_wb[k])
        for tc in range(NCH):
            qk_chunk(wb[0], R_wb[0], tc, 0 if fox else 2, QSq, R_QSq_lo, not fox)
            qk_chunk(wb[1], R_wb[1], tc, 1 if fox else 3, QSk, R_QSk_lo, not fox, load_cs=False)
        dma(QTo[0:64, :], QSq[64:128, :], reads=[R_QSq_lo], writes=[R_QTo_d])
        dma(KTo[0:64, :], QSk[64:128, :], reads=[R_QSk_lo], writes=[R_KTo_d])
        att_args = []
        for odd in (1, 0):
            h = 2 * j4 + odd
            QT, KT = (QTo, KTo) if odd else (QSq, QSk)
            R_Qd, R_Kd = (R_QTo_d, R_KTo_d) if odd else (R_QSq_lo, R_QSk_lo)
            R_Qx, R_Kx = (R_QTo_x, R_KTo_x) if odd else (R_QSq_hi, R_QSk_hi)
            xr = [R_QTo_d, R_KTo_d] if not odd else []
            if fox:
                op("pool", lambda e, QT=QT: e.memset(QT[64:96, :], 0.0), xr, [R_Qx], cols=S // 2)
                op("pool", lambda e, KT=KT: e.memset(KT[64:96, :], 0.0), xr, [R_Kx], cols=S // 2)
                dma(QT[64:67, :], CQ_d[h, :, :], reads=[R_CQd], writes=[R_Qx])
                dma(QT[67:70, :], ones3_d, writes=[R_Qx])
                dma(KT[64:67, :], ones3_d, writes=[R_Kx])
                dma(KT[67:70, :], CK_d[h, :, :], reads=[R_CKd], writes=[R_Kx])
            else:
                dma(KT[64:96, :], blk_d, reads=xr, writes=[R_Kx])
                op("dve", lambda e, KT=KT: e.tensor_reduce(kmf[0:64, 0:NB], KT[0:64, :].rearrange("p (n l) -> p n l", l=256),
                                                            AX.X, ALU.add), [R_Kd], [R_km], cols=S)
                ts("dve", kmb[0:64, 0:NB], kmf[0:64, 0:NB], 1.0 / 256, None, ALU.mult, None, [R_km], [R_km])
                gps = PSB[2][:].rearrange("p (a b) -> p a b", b=16)
                for qt in range(NT):
                    mm(gps[:, qt, :], QT[0:64, qt * 128:(qt + 1) * 128], kmb[0:64, :], True, True,
                       [R_Qd, R_km], [R_PS[2]])
                tt("dve", Gt, gps[:, 0:NT, :], CBt, ALU.add, [R_PS[2], R_CB], [R_G])
                for qt in range(NT):
                    op("dve", lambda e, qt=qt: e.max(top8[:, qt, :], Gt[:, qt, :]), [R_G], [R_top8[qt]], cols=16)
                tt("dve", LTt, Gt, top8[:, :, 2:3].to_broadcast([128, NT, 16]), ALU.is_lt, [R_G] + R_top8, [R_LT])
                tt("dve", mbp[:, :, 0:16], LTt, NEGMt, ALU.mult, [R_LT, R_CB], [R_mbp])
                for g in range(NT // 8):
                    pbk = PSB[3][0:32, :].bitcast(BF16).rearrange("p (a b) -> p a b", b=128)
                    for a in range(8):
                        qt = g * 8 + a
                        op("pe", lambda e, qt=qt, a=a, pbk=pbk: e.transpose(pbk[:, a, :], mbp[:, qt, :], ident),
                           [R_mbp, R_cbf], [R_PS[3]], cols=128)
                    op("act", lambda e, g=g, pbk=pbk: e.copy(out=MBT[:, g * 1024:(g + 1) * 1024].rearrange("p (a b) -> p a b", b=128),
                                                             in_=pbk), [R_PS[3]], [R_MBT], cols=1024)
                dma(QT[64:96, :], MBT, reads=[R_MBT] + xr, writes=[R_Qx])
            att_args.append((pp, QT, R_Qd, R_Qx, KT, R_Kd, R_Kx, odd))
        for g in range(NT // 4):
            b = proj_bank()
            pv = PSB[b][:].rearrange("p (a b) -> p a b", b=128)
            for a in range(4):
                t_ = g * 4 + a
                for c in range(8):
                    mm(pv[:, a, :], hT[:, c, t_ * 128:(t_ + 1) * 128], wb[2][:, c, :], c == 0, c == 7,
                       [R_wb[2], R_hT[t_]], [R_PS[b]])
            op("act", lambda e, pv=pv, g=g: e.copy(out=Vpe[:, g * 4:g * 4 + 4, 0:64], in_=pv[:, :, 0:64]),
               [R_PS[b]], [R_Vp], cols=256)
            op("dve", lambda e, pv=pv, g=g: e.tensor_copy(Vpo[:, g * 4:g * 4 + 4, 64:128], pv[:, :, 64:128]),
               [R_PS[b]], [R_Vp], cols=256)
        for tc in range(NCH):
            b = project_fm(wb[3], R_wb[3], tc)
            act(SG[:, tc * 512:(tc + 1) * 512], PSB[b][:], AF.Silu, [R_PS[b]], [R_SG])
        for a_ in att_args:
            attention(*a_)
    Sc.barrier()
    cv.off = 0
    xt2 = [cv([128, D]) for _ in range(2)]
    x1t = [cv([128, D]) for _ in range(2)]
    R_xt2, R_x1t = [Res(), Res()], [Res(), Res()]
    tmpC = ([cv([128, D], BF16) for _ in range(2)], [cv([128, 1]) for _ in range(2)],
            [cv([128, 1]) for _ in range(2)], [cv([128, D], BF16) for _ in range(2)],
            {k: [Res(), Res()] for k in ("sq", "ss", "rs", "hb")})
    wst2 = [cv([128, 8, 128]) for _ in range(2)]
    R_wst2 = [Res(), Res()]
    wo0b = cv([128, 8, D], BF16)
    R_wo0b = [Res() for _ in range(8)]
    mgl = [cv([128, 8, 512], BF16) for _ in range(2)]
    R_mgl = [Res(), Res()]
    R_X1d = [Res() for _ in range(NT)]
    for n in range(8):
        load_w(wo0b[:, :, n * 128:(n + 1) * 128], wo0_d[:, n * 128:(n + 1) * 128], None, wst2[n % 2], R_wst2[n % 2],
               R_wo0b[n], eng="pool" if n % 2 else "dve")
    for tc in range(NCH):
        m = tc % 2
        mgv = MG_d[:, :, tc * 512:(tc + 1) * 512].rearrange("a p s -> p a s")
        dma(mgl[m][:, 0:4, :], mgv[:, 0:4, :], reads=[R_MGd[pp][tc] for pp in range(8)], writes=[R_mgl[m]])
        dma(mgl[m][:, 4:8, :], mgv[:, 4:8, :], reads=[R_MGd[pp][tc] for pp in range(8)], writes=[R_mgl[m]])
        for tl in range(4):
            ti = tc * 4 + tl
            b2 = ti % 2
            dma(xt2[b2], x_d[ti * 128:(ti + 1) * 128, :], writes=[R_xt2[b2]])
            for n in range(2):
                bk = 4 + (2 * ti + n) % 4
                for pp in range(8):
                    mm(PSB[bk][:], mgl[m][:, pp, tl * 128:(tl + 1) * 128], wo0b[:, pp, n * 512:(n + 1) * 512],
                       pp == 0, pp == 7, [R_mgl[m]] + R_wo0b[n * 4:n * 4 + 4], [R_PS[bk]])
                tt("dve", x1t[b2][:, n * 512:(n + 1) * 512], PSB[bk][:], xt2[b2][:, n * 512:(n + 1) * 512],
                   ALU.add, [R_PS[bk], R_xt2[b2]], [R_x1t[b2]])
            dma(X1_d[ti * 128:(ti + 1) * 128, :], x1t[b2], reads=[R_x1t[b2]], writes=[R_X1d[ti]])
            norm_transpose(ti, x1t[b2], R_x1t[b2], tmpC)
    Sc.barrier()
    if stop_after == "l0":
        for ti in range(NT):
            b2 = ti % 2
            dma(xt2[b2], X1_d[ti * 128:(ti + 1) * 128, :], reads=[R_X1d[ti]], writes=[R_xt2[b2]])
            dma(out_d[ti * 128:(ti + 1) * 128, :], xt2[b2], reads=[R_xt2[b2]])
        return nc, Sc

    cv.off = 0
    CV = cv([128, 8, S], BF16)
    R_CV = [[Res() for _ in range(NCH)] for _ in range(8)]
    NM1 = 7
    misc1 = [cv([128, 512]) for _ in range(NM1)]
    R_misc1 = [Res() for _ in range(NM1)]
    m1 = [0]

    def M1():
        m1[0] = (m1[0] + 1) % NM1
        return misc1[m1[0]], R_misc1[m1[0]]

    rstdT, nmrT = cv([128, 512]), cv([128, 512])
    R_rstdT, R_nmrT = Res(), Res()
    offD = cv.off
    PAD = 32
    UT = cv([128, PAD + S], BF16)
    R_UT = [Res() for _ in range(NCH + 1)]
    DG = [cv([128, 31, 128], BF16) for _ in range(2)]
    R_DG = [Res(), Res()]
    wst3 = [cv([128, 8, 128]) for _ in range(2)]
    R_wst3 = [Res(), Res()]
    wvb, wgb = cv([128, 8, 128], BF16), cv([128, 8, 128], BF16)
    R_wvb, R_wgb = Res(), Res()
    KD = 8
    accs = [cv([128, 512]) for _ in range(2)]
    R_accs = [Res(), Res()]
    op("pool", lambda e: e.memset(UT[:, 0:PAD], 0.0), [], [R_UT[0]])
    for cc in range(8):
        load_w(wvb, w1_d[:, cc * 128:(cc + 1) * 128], V_G1, wst3[0], R_wst3[0], R_wvb)
        load_w(wgb, w1_d[:, 1024 + cc * 128:1024 + (cc + 1) * 128], V_G1, wst3[1], R_wst3[1], R_wgb)
        dg = DG[cc % 2]
        for j in range(KD, 31):
            col = V_CW + cc * 31 + j
            act(dg[:, j, :], ident, AF.Copy, [R_cbf, R_vec], [R_DG[cc % 2]], scale=vec[:, col:col + 1])
        for tc in range(NCH):
            b1 = project_fm(wvb, R_wvb, tc)
            b2 = project_fm(wgb, R_wgb, tc)
            sg, R_sg = M1()
            act(sg, PSB[b2][:], AF.Sigmoid, [R_PS[b2]], [R_sg])
            tt("dve", UT[:, PAD + tc * 512:PAD + (tc + 1) * 512], PSB[b1][:], sg, ALU.mult, [R_PS[b1], R_sg], [R_UT[tc + 1]])
        for tc in range(NCH):
            bk = 5 + tc % 3
            ac, R_ac = accs[tc % 2], R_accs[tc % 2]
            for j in range(KD):
                o0 = tc * 512 + j + PAD - 30
                col = V_CW + cc * 31 + j
                if j == 0:
                    ts("dve", ac, UT[:, o0:o0 + 512], vec[:, col:col + 1], None, ALU.mult, None,
                       [R_UT[tc], R_UT[tc + 1], R_vec], [R_ac])
                else:
                    op("dve", lambda e, o0=o0, col=col, ac=ac: e.scalar_tensor_tensor(
                        ac, UT[:, o0:o0 + 512], vec[:, col:col + 1], ac, ALU.mult, ALU.add),
                       [R_UT[tc], R_UT[tc + 1], R_vec, R_ac], [R_ac], cost=850.0)
            for j in range(KD, 31):
                o0 = tc * 512 + j + PAD - 30
                mm(PSB[bk][:], dg[:, j, :], UT[:, o0:o0 + 512], j == KD, j == 30,
                   [R_DG[cc % 2], R_UT[tc], R_UT[tc + 1]], [R_PS[bk]])
            op("dve", lambda e, bk=bk, tc=tc, ac=ac, cc=cc: e.scalar_tensor_tensor(
                CV[:, cc, tc * 512:(tc + 1) * 512], PSB[bk][:], vec[:, V_CB + cc:V_CB + cc + 1], ac, ALU.add, ALU.add),
               [R_PS[bk], R_vec, R_ac], [R_CV[cc][tc]])
    Sc.barrier()

    cv.off = offD
    w1zb = cv([128, 8, D], BF16)
    wo1b = cv([128, 8, D], BF16)
    R_w1zb = [Res() for _ in range(8)]
    R_wo1b = [Res() for _ in range(8)]
    wst4 = [cv([128, 8, 128]) for _ in range(2)]
    R_wst4 = [Res(), Res()]
    U2 = cv([128, 8, 512], BF16)
    R_U2 = [Res() for _ in range(8)]
    x1l = [cv([128, D]) for _ in range(2)]
    ot = x1l
    R_x1l = [Res(), Res()]
    R_ot = R_x1l
    sqv = [cv([128, 512], BF16) for _ in range(2)]
    R_sqv = [Res(), Res()]
    for n in range(8):
        load_w(w1zb[:, :, n * 128:(n + 1) * 128], w1_d[:, 2048 + n * 128:2048 + (n + 1) * 128], V_G1, wst4[0], R_wst4[0],
               R_w1zb[n], eng="pool")
        load_w(wo1b[:, :, n * 128:(n + 1) * 128], wo1_d[:, n * 128:(n + 1) * 128], None, wst4[1], R_wst4[1],
               R_wo1b[n], eng="dve")
    for tc in range(NCH):
        sl = slice(tc * 512, (tc + 1) * 512)
        for cc in range(8):
            act(sqv[cc % 2], CV[:, cc, sl], AF.Square, [R_CV[cc][tc]], [R_sqv[cc % 2]])
            mm(PSB[2][:], omean, CV[:, cc, sl], cc == 0, cc == 7, [R_cbf, R_CV[cc][tc]], [R_PS[2]])
            mm(PSB[3][:], omean, sqv[cc % 2], cc == 0, cc == 7, [R_cbf, R_sqv[cc % 2]], [R_PS[3]])
        m2, R_m2 = M1()
        act(m2, PSB[2][:], AF.Square, [R_PS[2]], [R_m2])
        var, R_var = M1()
        tt("dve", var, PSB[3][:], m2, ALU.subtract, [R_PS[3], R_m2], [R_var])
        act(var, var, AF.Sqrt, [R_var], [R_var], bias=1e-5, scale=1.0)
        op("dve", lambda e, var=var: e.reciprocal(rstdT, var), [R_var], [R_rstdT], cost=3300.0)
        tt("dve", nmrT, PSB[2][:], rstdT, ALU.mult, [R_PS[2], R_rstdT], [R_nmrT])
        for cc in range(8):
            t1, R_t1 = M1()
            tt("dve", t1, CV[:, cc, sl], rstdT, ALU.mult, [R_CV[cc][tc], R_rstdT], [R_t1])
            t2, R_t2 = M1()
            tt("pool", t2, t1, nmrT, ALU.subtract, [R_t1, R_nmrT], [R_t2])
            a_, R_a = M1()
            act(a_, t2, AF.Silu, [R_t2, R_vec], [R_a], bias=vec[:, V_LB + cc:V_LB + cc + 1],
                scale=vec[:, V_LG + cc:V_LG + cc + 1])
            bz = project_fm(w1zb[:, :, cc * 128:(cc + 1) * 128], R_w1zb[cc], tc)
            sz, R_sz = M1()
            act(sz, PSB[bz][:], AF.Silu, [R_PS[bz]], [R_sz])
            tt("pool", U2[:, cc, :], a_, sz, ALU.mult, [R_a, R_sz], [R_U2[cc]])
        for tl in range(4):
            ti = tc * 4 + tl
            b2 = ti % 2
            dma(x1l[b2], X1_d[ti * 128:(ti + 1) * 128, :], reads=[R_X1d[ti]], writes=[R_x1l[b2]])
            for n in range(2):
                bk = 5 + (2 * ti + n) % 3
                for cc in range(8):
                    mm(PSB[bk][:], U2[:, cc, tl * 128:(tl + 1) * 128], wo1b[:, cc, n * 512:(n + 1) * 512],
                       cc == 0, cc == 7, [R_U2[cc]] + R_wo1b[n * 4:n * 4 + 4], [R_PS[bk]])
                tt("dve", ot[b2][:, n * 512:(n + 1) * 512], PSB[bk][:], x1l[b2][:, n * 512:(n + 1) * 512],
                   ALU.add, [R_PS[bk], R_x1l[b2]], [R_ot[b2]])
            dma(out_d[ti * 128:(ti + 1) * 128, :], ot[b2], reads=[R_ot[b2]])
    return nc, Sc


_CACHE = {}


def _prep_inputs(inp, S):
    c = host_consts(S)
    shared = dict(c)
    shared["w_in0"] = np.ascontiguousarray(inp["l0_w_in"], np.float32)
    shared["w_out0"] = np.ascontiguousarray(inp["l0_w_out"], np.float32)
    shared["w_in1"] = np.ascontiguousarray(inp["l1_w_in"], np.float32)
    shared["w_out1"] = np.ascontiguousarray(inp["l1_w_out"], np.float32)
    shared["vec"] = host_vec(inp)
    return shared


def run(inp, S, ncores, stop_after=None, trace=False):
    nc, Sc = build(S, stop_after)
    with nc.Block() as block:
        import os as _os
        Sc.emit(block, reorder=(_os.environ.get('NOREORDER') is None))
    shared = _prep_inputs(inp, S)
    x = np.asarray(inp["x"], np.float32)
    in_maps = [dict(shared, x=np.ascontiguousarray(x[b])) for b in range(ncores)]
    res = run_bass_kernel_spmd(nc, in_maps, core_ids=list(range(ncores)), trace=trace)
    out = np.stack([np.asarray(r["out"], np.float32) for r in res.results], axis=0)
    return out, res


def kernel(**inputs):
    out, _ = run(inputs, 4096, NCORES)
    return out
```
